# Optimizing a Trainium2 kernel written in Bass

```python
import jax
import jax.numpy as jnp
from jax import lax
import numpy as np

D_MODEL = 1024
BATCH = 2
SEQ = 8192
DEPTH = 2

HEAD_DIM = 64
N_BRANCH = 4
BRANCH_HEADS = 4
BRANCH_WIDTH = BRANCH_HEADS * HEAD_DIM
ROPE_THETA = 10000.0
QBLK = 128
NEG = -1e30
TINY = 1e-30
EPS = 1e-6
ATTN_SCALE = HEAD_DIM ** -0.5
A_LATENT = 128
IDX_HEADS = 8
IDX_DIM = 32
DSA_TOPK = 256
MOBA_BLOCK = 256
MOBA_TOPK = 3
CMP_LEN = 32
CMP_STRIDE = 16
CMP_HIDDEN = 128
SLC_BLOCK = 64
SLC_TOPK = 16
NSA_WINDOW = 512
FORCE_SCORE = 1e9
NSA_KV = 6
SWA_WINDOW = 128
D_KV_HEADS = 2
D_FF = 256 * ((8 * D_MODEL // 3 + 255) // 256)
CONV_W = 3

IN_WIDTHS = (
    BRANCH_WIDTH,
    A_LATENT,
    IDX_HEADS * IDX_DIM,
    IDX_DIM,
    IDX_HEADS,
    BRANCH_WIDTH,
    BRANCH_WIDTH,
    BRANCH_WIDTH,
    BRANCH_WIDTH,
    NSA_KV * HEAD_DIM,
    3 * BRANCH_HEADS,
    BRANCH_WIDTH,
    D_KV_HEADS * HEAD_DIM,
    D_KV_HEADS * HEAD_DIM,
    N_BRANCH * D_MODEL,
)
N_IN = sum(IN_WIDTHS)

kernel_name = 'hybrid_dsa_moba_nsa_swa_convffn'


def rms_norm(x, g):
    xf = x.astype(jnp.float32)
    y = xf * lax.rsqrt(jnp.mean(jnp.square(xf), axis=-1, keepdims=True) + EPS)
    return (y * g.astype(jnp.float32)).astype(x.dtype)


def rope_tables(seq, dim):
    inv = ROPE_THETA ** (-jnp.arange(0, dim, 2, dtype=jnp.float32) / dim)
    ang = jnp.arange(seq, dtype=jnp.float32)[:, None] * inv[None, :]
    return jnp.cos(ang), jnp.sin(ang)


def apply_rope(x, cs):
    cos, sin = cs
    shape = (1, cos.shape[0]) + (1,) * (x.ndim - 3) + (cos.shape[1],)
    c = cos.reshape(shape).astype(x.dtype)
    s = sin.reshape(shape).astype(x.dtype)
    x1, x2 = jnp.split(x, 2, axis=-1)
    return jnp.concatenate([x1 * c - x2 * s, x2 * c + x1 * s], axis=-1)


def masked_softmax(s, ok, sink=None):
    s = jnp.where(ok, s, NEG)
    m = jnp.max(s, axis=-1, keepdims=True)
    if sink is not None:
        m = jnp.maximum(m, sink)
    e = jnp.where(ok, jnp.exp(s - m), 0.0)
    den = jnp.sum(e, axis=-1, keepdims=True)
    if sink is not None:
        den = den + jnp.exp(sink - m)
    return e / jnp.maximum(den, TINY)


def gather_rows(table, idx):
    return jax.vmap(lambda t, i: t[i])(table, idx)


def banded_attention(q, k, v, window, sink=None):
    B, S, KV, G, dh = q.shape
    nb = S // QBLK
    nprev = window // QBLK
    nk = (nprev + 1) * QBLK
    padw = ((0, 0), (nprev * QBLK, 0), (0, 0), (0, 0))
    kp = jnp.pad(k, padw).reshape(B, nb + nprev, QBLK, KV, dh)
    vp = jnp.pad(v, padw).reshape(B, nb + nprev, QBLK, KV, dh)
    kband = jnp.concatenate([kp[:, o:o + nb] for o in range(nprev + 1)], axis=2)
    vband = jnp.concatenate([vp[:, o:o + nb] for o in range(nprev + 1)], axis=2)
    qb = q.reshape(B, nb, QBLK, KV, G, dh)
    s = jnp.einsum('bnqcgd,bnkcd->bncgqk', qb, kband, preferred_element_type=jnp.float32) * ATTN_SCALE
    blk = jnp.arange(nb)[:, None]
    tpos = blk * QBLK + jnp.arange(QBLK)[None, :]
    kpos = (blk - nprev) * QBLK + jnp.arange(nk)[None, :]
    diff = tpos[:, :, None] - kpos[:, None, :]
    ok = (diff >= 0) & (diff < window) & (kpos[:, None, :] >= 0)
    ok = ok[None, :, None, None]
    sk = None if sink is None else sink.astype(jnp.float32)[None, None, :, :, None, None]
    p = masked_softmax(s, ok, sk)
    o = jnp.einsum('bncgqk,bnkcd->bnqcgd', p.astype(v.dtype), vband)
    return o.reshape(B, S, KV, G, dh)


def dsa_attention(q, k, v, iq, ik, iw):
    B, S, H, dh = q.shape
    topk = min(DSA_TOPK, S // 4)
    kpos = jnp.arange(S)
    ikf = ik.astype(jnp.float32)

    def one_block(i):
        t0 = i * QBLK
        qb = lax.dynamic_slice_in_dim(q, t0, QBLK, axis=1)
        iqb = lax.dynamic_slice_in_dim(iq, t0, QBLK, axis=1).astype(jnp.float32)
        iwb = lax.dynamic_slice_in_dim(iw, t0, QBLK, axis=1).astype(jnp.float32)
        tpos = t0 + jnp.arange(QBLK)
        dots = jnp.einsum('bqhd,bkd->bqkh', iqb, ikf)
        score = jnp.einsum('bqkh,bqh->bqk', jax.nn.relu(dots), iwb)
        score = jnp.where(kpos[None, None, :] <= tpos[None, :, None], score, NEG)
        _, sel = lax.top_k(score, topk)
        ks = gather_rows(k, sel)
        vs = gather_rows(v, sel)
        s = jnp.einsum('bqhd,bqkd->bhqk', qb, ks, preferred_element_type=jnp.float32) * ATTN_SCALE
        ok = (sel <= tpos[None, :, None])[:, None]
        p = masked_softmax(s, ok)
        return jnp.einsum('bhqk,bqkd->bqhd', p.astype(v.dtype), vs)

    out = lax.map(one_block, jnp.arange(S // QBLK))
    return out.transpose(1, 0, 2, 3, 4).reshape(B, S, H, dh)


def moba_attention(q, k, v):
    B, S, H, dh = q.shape
    nblk = -(-S // MOBA_BLOCK)
    pad = nblk * MOBA_BLOCK - S
    padw = ((0, 0), (0, pad), (0, 0), (0, 0))
    kp = jnp.pad(k, padw)
    vp = jnp.pad(v, padw)
    nsel = min(MOBA_TOPK, nblk - 1)
    kbh = kp.reshape(B, nblk, MOBA_BLOCK, H, dh).transpose(0, 3, 1, 2, 4)
    vbh = vp.reshape(B, nblk, MOBA_BLOCK, H, dh).transpose(0, 3, 1, 2, 4)
    kmean = jnp.mean(kbh.astype(jnp.float32), axis=3)
    blk_ids = jnp.arange(nblk)

    def one_block(i):
        t0 = i * QBLK
        qb = lax.dynamic_slice_in_dim(q, t0, QBLK, axis=1)
        tpos = t0 + jnp.arange(QBLK)
        own = t0 // MOBA_BLOCK
        ko = lax.dynamic_slice_in_dim(kp, own * MOBA_BLOCK, MOBA_BLOCK, axis=1)
        vo = lax.dynamic_slice_in_dim(vp, own * MOBA_BLOCK, MOBA_BLOCK, axis=1)
        opos = own * MOBA_BLOCK + jnp.arange(MOBA_BLOCK)
        s_own = jnp.einsum('bqhd,bkhd->bhqk', qb, ko, preferred_element_type=jnp.float32) * ATTN_SCALE
        ok_own = jnp.broadcast_to((opos[None, :] <= tpos[:, None])[None, None], s_own.shape)
        if nsel == 0:
            p = masked_softmax(s_own, ok_own)
            return jnp.einsum('bhqk,bkhd->bqhd', p.astype(v.dtype), vo)
        gate = jnp.einsum('bqhd,bhjd->bhqj', qb.astype(jnp.float32), kmean)
        gate = jnp.where(blk_ids < own, gate, NEG)
        _, sel = lax.top_k(gate, nsel)
        ks = jax.vmap(gather_rows)(kbh, sel)
        vs = jax.vmap(gather_rows)(vbh, sel)
        s_sel = jnp.einsum('bqhd,bhqnkd->bhqnk', qb, ks, preferred_element_type=jnp.float32) * ATTN_SCALE
        s_sel = s_sel.reshape(B, H, QBLK, nsel * MOBA_BLOCK)
        ok_sel = jnp.repeat(sel < own, MOBA_BLOCK, axis=-1)
        p = masked_softmax(jnp.concatenate([s_own, s_sel], axis=-1),
                           jnp.concatenate([ok_own, ok_sel], axis=-1)).astype(v.dtype)
        p_own = p[..., :MOBA_BLOCK]
        p_sel = p[..., MOBA_BLOCK:].reshape(B, H, QBLK, nsel, MOBA_BLOCK)
        return (jnp.einsum('bhqk,bkhd->bqhd', p_own, vo)
                + jnp.einsum('bhqnk,bhqnkd->bqhd', p_sel, vs))

    out = lax.map(one_block, jnp.arange(S // QBLK))
    return out.transpose(1, 0, 2, 3, 4).reshape(B, S, H, dh)


def compress_blocks(blocks, pe, w1, w2):
    B, N, L, dh = blocks.shape
    z = (blocks + pe).reshape(B, N, L * dh)
    return jax.nn.gelu(z @ w1) @ w2


def dsa_mixer(q_raw, lat_raw, iq_raw, ik_raw, iw_raw, qk_g, lat_g, kv_up, idx_k_g, cs, cs_idx):
    B, S, _ = q_raw.shape
    q = apply_rope(rms_norm(q_raw.reshape(B, S, BRANCH_HEADS, HEAD_DIM), qk_g[0]), cs)
    kv = rms_norm(lat_raw, lat_g) @ kv_up
    k, v = jnp.split(kv, 2, axis=-1)
    k = apply_rope(rms_norm(k, qk_g[1]), cs)
    iq = apply_rope(iq_raw.reshape(B, S, IDX_HEADS, IDX_DIM), cs_idx)
    ik = apply_rope(rms_norm(ik_raw, idx_k_g), cs_idx)
    iw = iw_raw * (IDX_HEADS ** -0.5 * IDX_DIM ** -0.5)
    return dsa_attention(q, k, v, iq, ik, iw)


def moba_mixer(q_raw, k_raw, v_raw, qk_g, cs):
    B, S, _ = q_raw.shape
    hs = (B, S, BRANCH_HEADS, HEAD_DIM)
    q = apply_rope(rms_norm(q_raw.reshape(hs), qk_g[0]), cs)
    k = apply_rope(rms_norm(k_raw.reshape(hs), qk_g[1]), cs)
    return moba_attention(q, k, v_raw.reshape(hs))


def nsa_mixer(q_raw, kv_raw, g_raw, qk_g, cmp_pe, cmp_w1, cmp_w2, cs):
    B, S, _ = q_raw.shape
    H, dh = BRANCH_HEADS, HEAD_DIM
    q = apply_rope(rms_norm(q_raw.reshape(B, S, H, dh), qk_g[0]), cs)
    kv = kv_raw.reshape(B, S, NSA_KV, dh)
    kc_raw, vc_raw = kv[:, :, 0], kv[:, :, 1]
    k_slc = apply_rope(rms_norm(kv[:, :, 2], qk_g[2]), cs)
    v_slc = kv[:, :, 3]
    k_win = apply_rope(rms_norm(kv[:, :, 4], qk_g[3]), cs)
    v_win = kv[:, :, 5]
    ncmp = (S - CMP_LEN) // CMP_STRIDE + 1
    gidx = np.arange(ncmp)[:, None] * CMP_STRIDE + np.arange(CMP_LEN)[None, :]
    k_cmp = rms_norm(compress_blocks(apply_rope(kc_raw, cs)[:, gidx], cmp_pe[0], cmp_w1[0], cmp_w2[0]), qk_g[1])
    v_cmp = compress_blocks(vc_raw[:, gidx], cmp_pe[1], cmp_w1[1], cmp_w2[1])
    cmp_end = jnp.asarray(np.arange(ncmp) * CMP_STRIDE + CMP_LEN - 1)
    nslc = S // SLC_BLOCK
    ntop = min(SLC_TOPK, nslc)
    c_start = np.arange(ncmp) * CMP_STRIDE
    s_start = np.arange(nslc) * SLC_BLOCK
    cmp_to_slc = jnp.asarray(((c_start[:, None] < s_start[None, :] + SLC_BLOCK)
                              & (c_start[:, None] + CMP_LEN > s_start[None, :])).astype(np.float32))
    ks_blocks = k_slc.reshape(B, nslc, SLC_BLOCK, dh)
    vs_blocks = v_slc.reshape(B, nslc, SLC_BLOCK, dh)
    jj = jnp.arange(nslc)

    def one_block(i):
        t0 = i * QBLK
        qb = lax.dynamic_slice_in_dim(q, t0, QBLK, axis=1)
        tpos = t0 + jnp.arange(QBLK)
        s_c = jnp.einsum('bqhd,bnd->bhqn', qb, k_cmp, preferred_element_type=jnp.float32) * ATTN_SCALE
        p_c = masked_softmax(s_c, (cmp_end[None, :] <= tpos[:, None])[None, None])
        o_c = jnp.einsum('bhqn,bnd->bqhd', p_c.astype(q.dtype), v_cmp)
        imp = jnp.einsum('bhqn,nj->bqj', p_c, cmp_to_slc)
        cur = tpos // SLC_BLOCK
        forced = (jj[None, :] == 0) | (jj[None, :] == cur[:, None]) | (jj[None, :] == cur[:, None] - 1)
        imp = jnp.where(forced[None], FORCE_SCORE, imp)
        imp = jnp.where((jj[None, :] <= cur[:, None])[None], imp, NEG)
        _, sel = lax.top_k(imp, ntop)
        kg = gather_rows(ks_blocks, sel)
        vg = gather_rows(vs_blocks, sel)
        kpos = sel[..., None] * SLC_BLOCK + jnp.arange(SLC_BLOCK)
        ok = (kpos <= tpos[None, :, None, None]).reshape(B, 1, QBLK, ntop * SLC_BLOCK)
        s_s = jnp.einsum('bqhd,bqnkd->bhqnk', qb, kg, preferred_element_type=jnp.float32) * ATTN_SCALE
        p_s = masked_softmax(s_s.reshape(B, H, QBLK, ntop * SLC_BLOCK), ok)
        o_s = jnp.einsum('bhqnk,bqnkd->bqhd', p_s.reshape(B, H, QBLK, ntop, SLC_BLOCK).astype(q.dtype), vg)
        return o_c, o_s

    o_c, o_s = lax.map(one_block, jnp.arange(S // QBLK))
    o_c = o_c.transpose(1, 0, 2, 3, 4).reshape(B, S, H, dh)
    o_s = o_s.transpose(1, 0, 2, 3, 4).reshape(B, S, H, dh)
    o_w = banded_attention(q.reshape(B, S, 1, H, dh), k_win[:, :, None], v_win[:, :, None], NSA_WINDOW)
    o_w = o_w.reshape(B, S, H, dh)
    g = jax.nn.sigmoid(g_raw.reshape(B, S, H, 3).astype(jnp.float32)).astype(q.dtype)
    return g[..., 0:1] * o_c + g[..., 1:2] * o_s + g[..., 2:3] * o_w


def swa_mixer(q_raw, k_raw, v_raw, qk_g, sink, cs):
    B, S, _ = q_raw.shape
    G = BRANCH_HEADS // D_KV_HEADS
    q = apply_rope(rms_norm(q_raw.reshape(B, S, BRANCH_HEADS, HEAD_DIM), qk_g[0]), cs)
    q = q.reshape(B, S, D_KV_HEADS, G, HEAD_DIM)
    k = apply_rope(rms_norm(k_raw.reshape(B, S, D_KV_HEADS, HEAD_DIM), qk_g[1]), cs)
    v = v_raw.reshape(B, S, D_KV_HEADS, HEAD_DIM)
    o = banded_attention(q, k, v, SWA_WINDOW, sink.reshape(D_KV_HEADS, G))
    return o.reshape(B, S, BRANCH_HEADS, HEAD_DIM)


def conv_ffn(h, w_up, conv_w, conv_b, w_down):
    u = h @ w_up
    c = u.shape[-1]
    u = lax.conv_general_dilated(u, conv_w[:, None, :], window_strides=(1,), padding=((CONV_W - 1, 0),),
                                 dimension_numbers=('NWC', 'WIO', 'NWC'), feature_group_count=c) + conv_b
    gate, val = jnp.split(u, 2, axis=-1)
    return (jax.nn.silu(gate) * val) @ w_down


def setup_inputs(seed: int = 0) -> dict:
    key = jax.random.key(seed)
    ks = jax.random.split(key, 21)
    f32 = jnp.float32

    def nrm(k, shape, fan_in):
        return jax.random.normal(k, shape, f32) * (fan_in ** -0.5)

    def gain(k, shape):
        return 1.0 + 0.05 * jax.random.normal(k, shape, f32)

    return {
        'x': jax.random.normal(ks[0], (BATCH, SEQ, D_MODEL), f32),
        'norm1_g': gain(ks[1], (DEPTH, D_MODEL)),
        'w_in': nrm(ks[2], (DEPTH, D_MODEL, N_IN), D_MODEL),
        'a_qk_g': gain(ks[3], (DEPTH, 2, HEAD_DIM)),
        'a_lat_g': gain(ks[4], (DEPTH, A_LATENT)),
        'a_kv_up': nrm(ks[5], (DEPTH, A_LATENT, 2 * HEAD_DIM), A_LATENT),
        'a_idx_k_g': gain(ks[6], (DEPTH, IDX_DIM)),
        'b_qk_g': gain(ks[7], (DEPTH, 2, HEAD_DIM)),
        'c_qk_g': gain(ks[8], (DEPTH, 4, HEAD_DIM)),
        'c_cmp_pe': 0.1 * jax.random.normal(ks[9], (DEPTH, 2, CMP_LEN, HEAD_DIM), f32),
        'c_cmp_w1': nrm(ks[10], (DEPTH, 2, CMP_LEN * HEAD_DIM, CMP_HIDDEN), CMP_LEN * HEAD_DIM),
        'c_cmp_w2': nrm(ks[11], (DEPTH, 2, CMP_HIDDEN, HEAD_DIM), CMP_HIDDEN),
        'd_qk_g': gain(ks[12], (DEPTH, 2, HEAD_DIM)),
        'd_sink': 0.5 * jax.random.normal(ks[13], (DEPTH, BRANCH_HEADS), f32),
        'w_branch': nrm(ks[14], (DEPTH, N_BRANCH, BRANCH_WIDTH, D_MODEL), BRANCH_WIDTH),
        'w_out': nrm(ks[15], (DEPTH, D_MODEL, D_MODEL), D_MODEL),
        'norm2_g': gain(ks[16], (DEPTH, D_MODEL)),
        'w_up': nrm(ks[17], (DEPTH, D_MODEL, 2 * D_FF), D_MODEL),
        'conv_w': nrm(ks[18], (DEPTH, CONV_W, 2 * D_FF), CONV_W),
        'conv_b': 0.01 * jax.random.normal(ks[19], (DEPTH, 2 * D_FF), f32),
        'w_down': nrm(ks[20], (DEPTH, D_FF, D_MODEL), D_FF),
    }


def reference(x, norm1_g, w_in, a_qk_g, a_lat_g, a_kv_up, a_idx_k_g, b_qk_g, c_qk_g, c_cmp_pe,
              c_cmp_w1, c_cmp_w2, d_qk_g, d_sink, w_branch, w_out, norm2_g, w_up, conv_w, conv_b, w_down):
    B, S, _ = x.shape
    cs = rope_tables(S, HEAD_DIM)
    cs_idx = rope_tables(S, IDX_DIM)
    split_at = np.cumsum(IN_WIDTHS)[:-1].tolist()
    for l in range(DEPTH):
        h = rms_norm(x, norm1_g[l])
        proj = h @ w_in[l]
        (a_q, a_lat, a_iq, a_ik, a_iw, b_q, b_k, b_v, c_q, c_kv, c_g,
         d_q, d_k, d_v, g_br) = jnp.split(proj, split_at, axis=-1)
        ya = dsa_mixer(a_q, a_lat, a_iq, a_ik, a_iw, a_qk_g[l], a_lat_g[l], a_kv_up[l], a_idx_k_g[l], cs, cs_idx)
        yb = moba_mixer(b_q, b_k, b_v, b_qk_g[l], cs)
        yc = nsa_mixer(c_q, c_kv, c_g, c_qk_g[l], c_cmp_pe[l], c_cmp_w1[l], c_cmp_w2[l], cs)
        yd = swa_mixer(d_q, d_k, d_v, d_qk_g[l], d_sink[l], cs)
        ys = jnp.stack([ya, yb, yc, yd], axis=2).reshape(B, S, N_BRANCH, BRANCH_WIDTH)
        br = jnp.einsum('bsnc,ncd->bsnd', ys, w_branch[l])
        gates = jax.nn.sigmoid(g_br.reshape(B, S, N_BRANCH, D_MODEL))
        x = x + jnp.einsum('bsnd,bsnd->bsd', gates, br) @ w_out[l]
        x = x + conv_ffn(rms_norm(x, norm2_g[l]), w_up[l], conv_w[l], conv_b[l], w_down[l])
    return x
```

```python
import numpy as np
from contextlib import ExitStack
import ml_dtypes
import concourse.bass as bass
import concourse.mybir as mybir
from concourse.bass_utils import run_bass_kernel_spmd

F32 = mybir.dt.float32
BF16 = mybir.dt.bfloat16
AF = mybir.ActivationFunctionType
ALU = mybir.AluOpType
AX = mybir.AxisListType
NPBF = ml_dtypes.bfloat16

D = 1024
NIN = 6708
DFF = 2816
GROUP = 4
NEGM = -30000.0
EPS = 1e-6
KBIS = 18
SAME_ENGINE_SYNC = True
import os
DBG_STOP = float(os.environ.get('DBG_STOP', '99'))

KF_ROWS = 736
QF_ROWS = 1280
KV_COLS = 576


class Buf:
    __slots__ = ("name", "w", "r", "dsem", "dcnt", "excl")

    def __init__(self, name, excl=False):
        self.name = name
        self.excl = excl
        self.w = None
        self.r = []
        self.dsem = None
        self.dcnt = 0


class Sched:
    def __init__(self, nc, stack):
        self.nc = nc
        self.stack = stack
        self.eng = {"pe": nc.tensor, "act": nc.scalar, "dve": nc.vector, "pool": nc.gpsimd, "sp": nc.sync}
        self.sem = {k: stack.enter_context(nc.semaphore("s_" + k)) for k in self.eng}
        self.cnt = {k: 0 for k in self.eng}
        self.seen = {k: {} for k in self.eng}
        self.dsems = []
        self.nwait = 0

    def _wait(self, e, tok):
        sem, val, owner = tok
        if owner == e and (e == "pe" or not SAME_ENGINE_SYNC):
            return
        key = id(sem)
        if self.seen[e].get(key, 0) >= val:
            return
        self.eng[e].wait_ge(sem, val)
        self.nwait += 1
        self.seen[e][key] = val

    def _deps(self, e, reads, writes, dma_dsem=None):
        for b in reads:
            if b.w is not None:
                self._wait(e, b.w)
        for b in writes:
            if b.w is not None and not (dma_dsem is not None and b.w[0] is dma_dsem):
                self._wait(e, b.w)
            for r in b.r:
                self._wait(e, r)

    def _commit(self, tok, reads, writes):
        for b in reads:
            if len(b.r) > 24:
                last = {}
                for t in b.r:
                    last[id(t[0])] = t
                b.r = list(last.values())
            b.r.append(tok)
        for b in writes:
            b.w = tok
            b.r = []

    def op(self, e, fn, reads=(), writes=()):
        xr = [b for b in reads if b.excl and e != "pe"]
        if xr:
            writes = list(writes) + [b for b in xr if b not in writes]
        self._deps(e, reads, writes)
        ins = fn(self.eng[e])
        self.cnt[e] += 1
        ins.then_inc(self.sem[e], 1)
        self._commit((self.sem[e], self.cnt[e], e), reads, writes)
        return ins

    def dma(self, q, out, in_, reads=(), writes=(), sembuf=None):
        sb = sembuf if sembuf is not None else writes[0]
        if sb.dsem is None:
            sb.dsem = self.stack.enter_context(self.nc.semaphore("d_" + sb.name))
            self.dsems.append(sb)
        self._deps(q, reads, writes, dma_dsem=sb.dsem)
        ins = self.eng[q].dma_start(out=out, in_=in_)
        sb.dcnt += 16
        ins.then_inc(sb.dsem, 16)
        self._commit((sb.dsem, sb.dcnt, "dma"), reads, writes)
        return ins

    def barrier(self):
        for e in self.eng:
            for e2 in self.eng:
                if e2 != e and self.cnt[e2] > 0:
                    self._wait(e, (self.sem[e2], self.cnt[e2], e2))
            for sb in self.dsems:
                if sb.dcnt > 0:
                    self._wait(e, (sb.dsem, sb.dcnt, "dma"))

    def finish(self, bufs):
        for b in bufs:
            if b.w is not None:
                self._wait("sp", b.w)


class Ctx:
    def __init__(self, S_len):
        self.nc = bass.Bass("TRN2", target_bir_lowering=False)
        self.S = S_len
        self.NT = S_len // 128
        self.T = self.NT // GROUP
        self.NB = S_len // 256
        self.NSLC = S_len // 64
        self.NCMP = S_len // 16 - 1
        self.NCT = (self.NCMP + 127) // 128
        self.stack = ExitStack()
        self.sch = Sched(self.nc, self.stack)
        self.bufs = {}
        nc = self.nc
        self.PS_S = self.psum("PS_S", [128, 2, 512], F32)
        self.bPS_S = [Buf("PS_S0", True), Buf("PS_S1", True)]
        self.PS_O = self.psum("PS_O", [128, 2, 4, 256], F32)
        self.bPS_O = [Buf("PS_O0", True), Buf("PS_O1", True)]
        self.PS_T = self.psum("PS_T", [128, 1024], BF16)
        self.bPS_T = Buf("PS_T", True)
        self.PS_M = self.psum("PS_M", [128, 512], F32)
        self.bPS_M = Buf("PS_M", True)
        self.s_slot = 0
        self.o_slot = 0

    def din(self, name, shape, dt):
        return self.nc.dram_tensor(name, list(shape), dt, kind="ExternalInput").ap()

    def dout(self, name, shape, dt):
        return self.nc.dram_tensor(name, list(shape), dt, kind="ExternalOutput").ap()

    def sb(self, name, shape, dt, stack=None):
        st = stack if stack is not None else self.stack
        self.uid = getattr(self, "uid", 0) + 1
        return st.enter_context(self.nc.sbuf_tensor("sb%d_%s" % (self.uid, name), list(shape), dt))

    def psum(self, name, shape, dt):
        return self.stack.enter_context(self.nc.psum_tensor(name, list(shape), dt))

    def next_s(self):
        s = self.s_slot
        self.s_slot ^= 1
        return s

    def next_o(self):
        s = self.o_slot
        self.o_slot ^= 1
        return s


def load_consts(cx, st):
    S = cx.sch
    idd = cx.din("ident", [128, 128], F32)
    idf = cx.sb("idf", [128, 128], F32, st)
    idb = cx.sb("idb", [128, 128], BF16, st)
    b_idf, b_idb = Buf("idf"), Buf("idb")
    S.dma("sp", idf[:], idd[:, :], writes=[b_idf])
    S.op("dve", lambda e: e.tensor_copy(out=idb[:], in_=idf[:]), reads=[b_idf], writes=[b_idb])
    cx.idf, cx.idb, cx.b_idf, cx.b_idb = idf, idb, b_idf, b_idb


def rms_rows(cx, xt, bx, width, out_rstd, b_rstd, junk, b_junk, ss, b_ss, np_=128):
    S = cx.sch
    S.op("act", lambda e: e.activation(out=junk, in_=xt, func=AF.Square, accum_out=ss), reads=[bx], writes=[b_junk, b_ss])
    S.op("act", lambda e: e.activation(out=out_rstd, in_=ss, func=AF.Sqrt, scale=1.0 / width, bias=cx.eps_t[0:np_, 0:1]),
         reads=[b_ss, cx.b_eps], writes=[b_rstd])
    S.op("dve", lambda e: e.reciprocal(out=out_rstd, in_=out_rstd), reads=[b_rstd], writes=[b_rstd])


def make_eps(cx, st):
    S = cx.sch
    cx.eps_t = cx.sb("eps_t", [128, 1], F32, st)
    cx.b_eps = Buf("eps")
    S.op("dve", lambda e: e.memset(cx.eps_t[:], EPS), writes=[cx.b_eps])


def phase_a(cx, x_own, norm_g, w_in_p, gains, lat_g, ikg, kv_up, cs64, cs32,
            KF_own, KV_own, QF_own, IW_own, CG_own, SG_own):
    S = cx.sch
    T = cx.T
    with ExitStack() as st:
        sb = lambda n, s, d: cx.sb(n, s, d, st)
        win = sb("win", [128, 8, NIN], BF16); b_win = Buf("win")
        for k in range(8):
            S.dma("pool", win[:, k, :], w_in_p[k * 128:(k + 1) * 128, :], writes=[b_win])
        kvu = sb("kvu", [128, 128], BF16); b_kvu = Buf("kvu")
        S.dma("pool", kvu[:], kv_up[:, :], writes=[b_kvu])
        g1 = sb("g1", [128, D], F32); b_g1 = Buf("g1")
        S.dma("sp", g1[:], norm_g[0:1, :].to_broadcast([128, D]), writes=[b_g1])
        G = sb("G", [128, 26, 64], F32); b_G = Buf("G")
        S.dma("sp", G[:].rearrange("p a b -> p (a b)"), gains[0:1, :].to_broadcast([128, 26 * 64]), writes=[b_G])
        latg = sb("latg", [128, 128], F32); b_latg = Buf("latg")
        S.dma("sp", latg[:], lat_g[0:1, :].to_broadcast([128, 128]), writes=[b_latg])
        ikgt = sb("ikgt", [128, 32], F32); b_ikg = Buf("ikgt")
        S.dma("sp", ikgt[:], ikg[0:1, :].to_broadcast([128, 32]), writes=[b_ikg])

        xt = sb("xt", [128, D], F32); b_xt = Buf("xt")
        junk = sb("junk", [128, D], F32); b_junk = Buf("junk")
        ss = sb("ss", [128, 1], F32); b_ss = Buf("ss")
        rstd = sb("rstd", [128, 1], F32); b_rstd = Buf("rstd")
        h = sb("h", [128, D], BF16); b_h = Buf("h")
        hT = sb("hT", [128, 8, 128], BF16); b_hT = Buf("hT")
        R = sb("R", [128, 26, 64], F32); b_R = Buf("R")
        TQ = sb("TQ", [128, 26, 64], F32); b_TQ = Buf("TQ")
        T1 = sb("T1", [128, 26, 32], F32); b_T1 = Buf("T1")
        T2 = sb("T2", [128, 26, 32], F32); b_T2 = Buf("T2")
        ssr = sb("ssr", [128, 26], F32); b_ssr = Buf("ssr")
        Rb = sb("Rb", [128, 26, 64], BF16); b_Rb = Buf("Rb")
        RT = sb("RT", [128, 13, 128], BF16); b_RT = Buf("RT")
        Vt = sb("Vt", [128, 640], BF16); b_Vt = Buf("Vt")
        LAT = sb("LAT", [128, 128], F32); b_LAT = Buf("LAT")
        latb = sb("latb", [128, 128], BF16); b_latb = Buf("latb")
        latT = sb("latT", [128, 128], BF16); b_latT = Buf("latT")
        IQ = sb("IQ", [128, 9, 32], F32); b_IQ = Buf("IQ")
        IQb = sb("IQb", [128, 9, 32], BF16); b_IQb = Buf("IQb")
        IQT = sb("IQT", [128, 3, 128], BF16); b_IQT = Buf("IQT")
        U1 = sb("U1", [128, 9, 16], F32); b_U1 = Buf("U1")
        U2 = sb("U2", [128, 9, 16], F32); b_U2 = Buf("U2")
        IW = sb("IW", [128, 8], F32); b_IW = Buf("IW")
        CG = sb("CG", [128, 12], F32); b_CG = Buf("CG")
        SG = sb("SG", [128, 4096], BF16); b_SG = Buf("SG")
        vcT = sb("vcT", [64, 128], BF16); b_vcT = Buf("vcT")
        c64 = sb("c64", [128, 2, 32], F32); b_c64 = Buf("c64")
        c32 = sb("c32", [128, 2, 16], F32); b_c32 = Buf("c32")
        b_KF, b_KV, b_QF, b_IWd, b_CGd, b_SGd = Buf("KFd"), Buf("KVd"), Buf("QFd"), Buf("IWd"), Buf("CGd"), Buf("SGd")

        for m in range(T):
            rows = slice(m * 128, (m + 1) * 128)
            S.dma("sp", xt[:], x_own[rows, :], writes=[b_xt])
            S.dma("sp", c64[:], cs64[rows, :].rearrange("p (a b) -> p a b", a=2), writes=[b_c64])
            S.dma("sp", c32[:], cs32[rows, :].rearrange("p (a b) -> p a b", a=2), writes=[b_c32])
            if DBG_STOP <= -3:
                continue
            rms_rows(cx, xt[:], b_xt, D, rstd[:], b_rstd, junk[:], b_junk, ss[:], b_ss)
            S.op("dve", lambda e: e.scalar_tensor_tensor(out=h[:], in0=xt[:], scalar=rstd[:, 0:1], in1=g1[:], op0=ALU.mult, op1=ALU.mult),
                 reads=[b_xt, b_rstd, b_g1], writes=[b_h])
            for k in range(8):
                S.op("pe", lambda e: e.transpose(out=cx.PS_T[:, k * 128:(k + 1) * 128], in_=h[:, k * 128:(k + 1) * 128], identity=cx.idb[:]),
                     reads=[b_h, cx.b_idb], writes=[cx.bPS_T])
            S.op("act", lambda e: e.copy(out=hT[:].rearrange("p a b -> p (a b)"), in_=cx.PS_T[:, :]), reads=[cx.bPS_T], writes=[b_hT])

            if DBG_STOP <= -2:
                continue

            def proj(c0, c1):
                sl = cx.next_s()
                for k in range(8):
                    S.op("pe", lambda e: e.matmul(cx.PS_S[:, sl, 0:c1 - c0], lhsT=hT[:, k, :], rhs=win[:, k, c0:c1], start=(k == 0), stop=(k == 7)),
                         reads=[b_hT, b_win], writes=[cx.bPS_S[sl]])
                return sl

            Rf = R[:].rearrange("p a b -> p (a b)")
            IQf = IQ[:].rearrange("p a b -> p (a b)")
            sl = proj(0, 512)
            S.op("act", lambda e: e.copy(out=Rf[:, 0:512], in_=cx.PS_S[:, sl, 0:512]), reads=[cx.bPS_S[sl]], writes=[b_R])
            if DBG_STOP <= -1:
                continue
            sl = proj(512, 1024)
            S.op("act", lambda e: e.copy(out=Rf[:, 512:576], in_=cx.PS_S[:, sl, 0:64]), reads=[cx.bPS_S[sl]], writes=[b_R])
            S.op("dve", lambda e: e.tensor_copy(out=Rf[:, 640:1088], in_=cx.PS_S[:, sl, 64:512]), reads=[cx.bPS_S[sl]], writes=[b_R])
            if DBG_STOP <= -0.8:
                continue
            sl = proj(1024, 1536)
            S.op("act", lambda e: e.copy(out=Rf[:, 1088:1600], in_=cx.PS_S[:, sl, 0:512]), reads=[cx.bPS_S[sl]], writes=[b_R])
            if DBG_STOP <= -0.6:
                continue
            sl = proj(1536, 2048)
            S.op("act", lambda e: e.copy(out=Rf[:, 1600:1664], in_=cx.PS_S[:, sl, 0:64]), reads=[cx.bPS_S[sl]], writes=[b_R])
            if os.environ.get("V1") == "act":
                S.op("act", lambda e: e.copy(out=Vt[:, 0:448], in_=cx.PS_S[:, sl, 64:512]), reads=[cx.bPS_S[sl]], writes=[b_Vt])
            elif os.environ.get("V1") == "none":
                pass
            else:
                S.op("dve", lambda e: e.tensor_copy(out=Vt[:, 0:448], in_=cx.PS_S[:, sl, 64:512]), reads=[cx.bPS_S[sl]], writes=[b_Vt])
            if DBG_STOP <= -0.4:
                continue
            sl = proj(2048, 2560)
            S.op("act", lambda e: e.copy(out=Vt[:, 448:512], in_=cx.PS_S[:, sl, 0:64]), reads=[cx.bPS_S[sl]], writes=[b_Vt])
            S.op("act", lambda e: e.copy(out=Vt[:, 576:640], in_=cx.PS_S[:, sl, 64:128]), reads=[cx.bPS_S[sl]], writes=[b_Vt])
            S.op("dve", lambda e: e.tensor_copy(out=LAT[:], in_=cx.PS_S[:, sl, 128:256]), reads=[cx.bPS_S[sl]], writes=[b_LAT])
            S.op("dve", lambda e: e.tensor_copy(out=IQf[:, 0:256], in_=cx.PS_S[:, sl, 256:512]), reads=[cx.bPS_S[sl]], writes=[b_IQ])
            if DBG_STOP <= -0.2:
                continue
            sl = proj(2560, 2612)
            S.op("act", lambda e: e.copy(out=IQf[:, 256:288], in_=cx.PS_S[:, sl, 0:32]), reads=[cx.bPS_S[sl]], writes=[b_IQ])
            S.op("act", lambda e: e.activation(out=IW[:], in_=cx.PS_S[:, sl, 32:40], func=AF.Copy, scale=1.0 / 16.0), reads=[cx.bPS_S[sl]], writes=[b_IW])
            S.op("act", lambda e: e.activation(out=CG[:], in_=cx.PS_S[:, sl, 40:52], func=AF.Sigmoid), reads=[cx.bPS_S[sl]], writes=[b_CG])
            if DBG_STOP <= 0:
                continue
            for gch in range(8):
                sl = proj(2612 + gch * 512, 2612 + (gch + 1) * 512)
                S.op("act", lambda e: e.activation(out=SG[:, gch * 512:(gch + 1) * 512], in_=cx.PS_S[:, sl, :], func=AF.Sigmoid),
                     reads=[cx.bPS_S[sl]], writes=[b_SG])
            S.dma("sp", SG_own[rows, :], SG[:], reads=[b_SG], writes=[b_SGd], sembuf=b_SG)
            S.dma("sp", IW_own[rows, :], IW[:], reads=[b_IW], writes=[b_IWd], sembuf=b_IW)
            S.dma("sp", CG_own[rows, :], CG[:], reads=[b_CG], writes=[b_CGd], sembuf=b_CG)

            if DBG_STOP <= 1:
                continue
            rms_rows(cx, LAT[:], b_LAT, 128, rstd[:], b_rstd, junk[:, 0:128], b_junk, ss[:], b_ss)
            S.op("dve", lambda e: e.scalar_tensor_tensor(out=latb[:], in0=LAT[:], scalar=rstd[:, 0:1], in1=latg[:], op0=ALU.mult, op1=ALU.mult),
                 reads=[b_LAT, b_rstd, b_latg], writes=[b_latb])
            S.op("pe", lambda e: e.transpose(out=cx.PS_T[:, 0:128], in_=latb[:], identity=cx.idb[:]), reads=[b_latb, cx.b_idb], writes=[cx.bPS_T])
            S.op("act", lambda e: e.copy(out=latT[:], in_=cx.PS_T[:, 0:128]), reads=[cx.bPS_T], writes=[b_latT])
            S.op("pe", lambda e: e.matmul(cx.PS_M[:, 0:128], lhsT=latT[:], rhs=kvu[:], start=True, stop=True), reads=[b_latT, b_kvu], writes=[cx.bPS_M])
            S.op("act", lambda e: e.copy(out=R[:, 9, :], in_=cx.PS_M[:, 0:64]), reads=[cx.bPS_M], writes=[b_R])
            S.op("act", lambda e: e.copy(out=Vt[:, 512:576], in_=cx.PS_M[:, 64:128]), reads=[cx.bPS_M], writes=[b_Vt])

            if DBG_STOP <= 2:
                continue
            TQf = TQ[:].rearrange("p a b -> p (a b)")
            S.op("pool", lambda e: e.tensor_tensor(out=TQf, in0=Rf, in1=Rf, op=ALU.mult), reads=[b_R], writes=[b_TQ])
            S.op("dve", lambda e: e.tensor_reduce(out=ssr[:], in_=TQ[:], op=ALU.add, axis=AX.X), reads=[b_TQ], writes=[b_ssr])
            S.op("act", lambda e: e.activation(out=ssr[:], in_=ssr[:], func=AF.Sqrt, scale=1.0 / 64, bias=cx.eps_t[:, 0:1]), reads=[b_ssr, cx.b_eps], writes=[b_ssr])
            S.op("dve", lambda e: e.reciprocal(out=ssr[:], in_=ssr[:]), reads=[b_ssr], writes=[b_ssr])
            S.op("dve", lambda e: e.memset(ssr[:, 4:5], 1.0), writes=[b_ssr])
            S.op("dve", lambda e: e.tensor_tensor(out=TQ[:], in0=R[:], in1=ssr[:].unsqueeze(2).to_broadcast([128, 26, 64]), op=ALU.mult),
                 reads=[b_R, b_ssr], writes=[b_TQ])
            S.op("pool", lambda e: e.tensor_tensor(out=TQf, in0=TQf, in1=G[:].rearrange("p a b -> p (a b)"), op=ALU.mult), reads=[b_TQ, b_G], writes=[b_TQ])
            cosb = c64[:, 0:1, :].to_broadcast([128, 26, 32])
            sinb = c64[:, 1:2, :].to_broadcast([128, 26, 32])
            lo, hi = TQ[:, :, 0:32], TQ[:, :, 32:64]
            S.op("dve", lambda e: e.tensor_tensor(out=T1[:], in0=lo, in1=cosb, op=ALU.mult), reads=[b_TQ, b_c64], writes=[b_T1])
            S.op("pool", lambda e: e.tensor_tensor(out=T2[:], in0=hi, in1=sinb, op=ALU.mult), reads=[b_TQ, b_c64], writes=[b_T2])
            S.op("dve", lambda e: e.tensor_tensor(out=Rb[:, :, 0:32], in0=T1[:], in1=T2[:], op=ALU.subtract), reads=[b_T1, b_T2], writes=[b_Rb])
            S.op("dve", lambda e: e.tensor_tensor(out=T1[:], in0=hi, in1=cosb, op=ALU.mult), reads=[b_TQ, b_c64], writes=[b_T1])
            S.op("pool", lambda e: e.tensor_tensor(out=T2[:], in0=lo, in1=sinb, op=ALU.mult), reads=[b_TQ, b_c64], writes=[b_T2])
            S.op("dve", lambda e: e.tensor_tensor(out=Rb[:, :, 32:64], in0=T1[:], in1=T2[:], op=ALU.add), reads=[b_T1, b_T2], writes=[b_Rb])
            Rbf = Rb[:].rearrange("p a b -> p (a b)")
            for p0 in (0, 8):
                npair = min(8, 13 - p0)
                for p in range(npair):
                    S.op("pe", lambda e: e.transpose(out=cx.PS_T[:, p * 128:(p + 1) * 128], in_=Rbf[:, (p0 + p) * 128:(p0 + p + 1) * 128], identity=cx.idb[:]),
                         reads=[b_Rb, cx.b_idb], writes=[cx.bPS_T])
                S.op("act", lambda e: e.copy(out=RT[:, p0:p0 + npair, :].rearrange("p a b -> p (a b)"), in_=cx.PS_T[:, 0:npair * 128]),
                     reads=[cx.bPS_T], writes=[b_RT])
            S.dma("sp", KF_own[0:640, rows].rearrange("(a q) t -> q a t", q=128), RT[:, 0:5, :], reads=[b_RT], writes=[b_KF], sembuf=b_RT)
            S.dma("sp", QF_own[0:1024, rows].rearrange("(a q) t -> q a t", q=128), RT[:, 5:13, :], reads=[b_RT], writes=[b_QF], sembuf=b_RT)
            S.op("pe", lambda e: e.transpose(out=cx.PS_T[0:64, 0:128], in_=Vt[:, 576:640], identity=cx.idb[:]), reads=[b_Vt, cx.b_idb], writes=[cx.bPS_T])
            S.op("act", lambda e: e.copy(out=vcT[:], in_=cx.PS_T[0:64, 0:128]), reads=[cx.bPS_T], writes=[b_vcT])
            S.dma("sp", KF_own[640:704, rows], vcT[:], reads=[b_vcT], writes=[b_KF], sembuf=b_vcT)
            S.dma("sp", KV_own[rows, :], Vt[:, 0:576], reads=[b_Vt], writes=[b_KV], sembuf=b_Vt)

            if DBG_STOP <= 3:
                continue
            S.op("act", lambda e: e.activation(out=junk[:, 0:32], in_=IQ[:, 8, :], func=AF.Square, accum_out=ss[:]), reads=[b_IQ], writes=[b_junk, b_ss])
            S.op("act", lambda e: e.activation(out=rstd[:], in_=ss[:], func=AF.Sqrt, scale=1.0 / 32, bias=cx.eps_t[:, 0:1]), reads=[b_ss, cx.b_eps], writes=[b_rstd])
            S.op("dve", lambda e: e.reciprocal(out=rstd[:], in_=rstd[:]), reads=[b_rstd], writes=[b_rstd])
            S.op("dve", lambda e: e.scalar_tensor_tensor(out=IQ[:, 8, :], in0=IQ[:, 8, :], scalar=rstd[:, 0:1], in1=ikgt[:], op0=ALU.mult, op1=ALU.mult),
                 reads=[b_IQ, b_rstd, b_ikg], writes=[b_IQ])
            cb = c32[:, 0:1, :].to_broadcast([128, 9, 16])
            sbb = c32[:, 1:2, :].to_broadcast([128, 9, 16])
            lo, hi = IQ[:, :, 0:16], IQ[:, :, 16:32]
            S.op("dve", lambda e: e.tensor_tensor(out=U1[:], in0=lo, in1=cb, op=ALU.mult), reads=[b_IQ, b_c32], writes=[b_U1])
            S.op("pool", lambda e: e.tensor_tensor(out=U2[:], in0=hi, in1=sbb, op=ALU.mult), reads=[b_IQ, b_c32], writes=[b_U2])
            S.op("dve", lambda e: e.tensor_tensor(out=IQb[:, :, 0:16], in0=U1[:], in1=U2[:], op=ALU.subtract), reads=[b_U1, b_U2], writes=[b_IQb])
            S.op("dve", lambda e: e.tensor_tensor(out=U1[:], in0=hi, in1=cb, op=ALU.mult), reads=[b_IQ, b_c32], writes=[b_U1])
            S.op("pool", lambda e: e.tensor_tensor(out=U2[:], in0=lo, in1=sbb, op=ALU.mult), reads=[b_IQ, b_c32], writes=[b_U2])
            S.op("dve", lambda e: e.tensor_tensor(out=IQb[:, :, 16:32], in0=U1[:], in1=U2[:], op=ALU.add), reads=[b_U1, b_U2], writes=[b_IQb])
            IQbf = IQb[:].rearrange("p a b -> p (a b)")
            S.op("pe", lambda e: e.transpose(out=cx.PS_T[:, 0:128], in_=IQbf[:, 0:128], identity=cx.idb[:]), reads=[b_IQb, cx.b_idb], writes=[cx.bPS_T])
            S.op("pe", lambda e: e.transpose(out=cx.PS_T[:, 128:256], in_=IQbf[:, 128:256], identity=cx.idb[:]), reads=[b_IQb, cx.b_idb], writes=[cx.bPS_T])
            S.op("pe", lambda e: e.transpose(out=cx.PS_T[0:32, 256:384], in_=IQbf[:, 256:288], identity=cx.idb[:]), reads=[b_IQb, cx.b_idb], writes=[cx.bPS_T])
            S.op("act", lambda e: e.copy(out=IQT[:, 0:2, :].rearrange("p a b -> p (a b)"), in_=cx.PS_T[:, 0:256]), reads=[cx.bPS_T], writes=[b_IQT])
            S.op("act", lambda e: e.copy(out=IQT[0:32, 2, :], in_=cx.PS_T[0:32, 256:384]), reads=[cx.bPS_T], writes=[b_IQT])
            S.dma("sp", QF_own[1024:1280, rows].rearrange("(a q) t -> q a t", q=128), IQT[:, 0:2, :], reads=[b_IQT], writes=[b_QF], sembuf=b_IQT)
            S.dma("sp", KF_own[704:736, rows], IQT[0:32, 2, :], reads=[b_IQT], writes=[b_KF], sembuf=b_IQT)
        S.barrier()
    return [b_KF, b_KV, b_QF, b_IWd, b_CGd, b_SGd]


def _perm_cols():
    r = np.r_
    return np.concatenate([
        r[936:1192], r[1704:1768], r[1832:1896], r[1960:2024], r[2356:2484], r[0:256], r[680:936], r[1448:1704], r[2100:2356],
        r[1192:1448], r[1896:1960], r[2024:2088], r[2484:2612], r[1768:1832],
        r[256:384], r[384:640], r[640:672], r[672:680], r[2088:2100],
        r[2612:6708]])


def _gains(a_qk_g, b_qk_g, c_qk_g, d_qk_g):
    rows = [b_qk_g[1]] * 4 + [np.ones(64, np.float32), c_qk_g[2], c_qk_g[3]] + [d_qk_g[1]] * 2 + [a_qk_g[1]] \
        + [a_qk_g[0]] * 4 + [b_qk_g[0]] * 4 + [c_qk_g[0]] * 4 + [d_qk_g[0]] * 4
    return np.ascontiguousarray(np.stack(rows).reshape(1, 26 * 64).astype(np.float32))


def _rope_table(S_len, dim):
    inv = (np.float32(10000.0) ** (-np.arange(0, dim, 2, dtype=np.float32) / np.float32(dim))).astype(np.float32)
    ang = (np.arange(S_len, dtype=np.float32)[:, None] * inv[None, :]).astype(np.float32)
    return np.concatenate([np.cos(ang), np.sin(ang)], axis=1).astype(np.float32)


def own_tokens(S_len, r):
    NT = S_len // 128
    T = NT // GROUP
    return np.concatenate([np.arange(128 * (GROUP * m + r), 128 * (GROUP * m + r) + 128) for m in range(T)])


def build_a(S_len):
    cx = Ctx(S_len)
    T = cx.T
    x_own = cx.din("x_own", [T * 128, D], F32)
    norm_g = cx.din("norm_g", [1, D], F32)
    w_in_p = cx.din("w_in_p", [D, NIN], F32)
    gains = cx.din("gains", [1, 26 * 64], F32)
    lat_g = cx.din("lat_g", [1, 128], F32)
    ikg = cx.din("ikg", [1, 32], F32)
    kv_up = cx.din("kv_up", [128, 128], F32)
    cs64 = cx.din("cs64", [T * 128, 64], F32)
    cs32 = cx.din("cs32", [T * 128, 32], F32)
    KF = cx.dout("KF", [KF_ROWS, T * 128], BF16)
    KV = cx.dout("KV", [T * 128, KV_COLS], BF16)
    QF = cx.dout("QF", [QF_ROWS, T * 128], BF16)
    IW = cx.dout("IW", [T * 128, 8], F32)
    CG = cx.dout("CG", [T * 128, 12], F32)
    SG = cx.dout("SG", [T * 128, 4096], BF16)
    load_consts(cx, cx.stack)
    make_eps(cx, cx.stack)
    outs = phase_a(cx, x_own, norm_g, w_in_p, gains, lat_g, ikg, kv_up, cs64, cs32, KF, KV, QF, IW, CG, SG)
    cx.sch.finish(outs)
    print("phase A instr", cx.sch.cnt, "waits", cx.sch.nwait)
    cx.stack.close()
    return cx.nc


def load_kf(cx, KF_all, dst, b_dst, row0, nrows):
    S = cx.sch
    for r in range(GROUP):
        d = dst.rearrange("d (m r t) -> d m r t", r=GROUP, t=128)[:, :, r, :]
        s = KF_all[r * KF_ROWS + row0:r * KF_ROWS + row0 + nrows, :].rearrange("d (m t) -> d m t", t=128)
        S.dma("sp", d, s, writes=[b_dst])


def load_kv(cx, KV_all, dst, b_dst, col0, ncols):
    S = cx.sch
    T = cx.T
    for r in range(GROUP):
        d = dst.rearrange("p (m r) c -> p m r c", r=GROUP)[:, :, r, :]
        s = KV_all[r * T * 128:(r + 1) * T * 128, col0:col0 + ncols].rearrange("(m t) c -> t m c", t=128)
        step = max(1, min(T, 8))
        for m0 in range(0, T, step):
            S.dma("sp", d[:, m0:m0 + step, :], s[:, m0:m0 + step, :], writes=[b_dst])


def attend(cx, units, vfun, W, acc, b_acc, scale=0.125):
    S = cx.sch
    CH = 8
    first = True
    for u0 in range(0, len(units), CH):
        chunk = units[u0:u0 + CH]
        ei = cx.e_slot
        cx.e_slot ^= 1
        E, bE = cx.ebufs[ei]
        for ci, (masks, qks) in enumerate(chunk):
            sl = cx.next_s()
            n = len(masks) + len(qks)
            k = 0
            for (l, r, rd) in masks:
                S.op("pe", lambda e: e.matmul(cx.PS_S[:, sl, :], lhsT=l, rhs=r, start=(k == 0), stop=(k == n - 1)), reads=rd, writes=[cx.bPS_S[sl]])
                k += 1
            for (c0, c1, l, r, rd) in qks:
                S.op("pe", lambda e: e.matmul(cx.PS_S[:, sl, c0:c1], lhsT=l, rhs=r, start=(k == 0), stop=(k == n - 1)), reads=rd, writes=[cx.bPS_S[sl]])
                k += 1
            if DBG_STOP <= 10:
                continue
            S.op("act", lambda e: e.activation(out=E[:, ci, :], in_=cx.PS_S[:, sl, :], func=AF.Exp, scale=scale), reads=[cx.bPS_S[sl]], writes=[bE])
        if DBG_STOP <= 11:
            continue
        osl = cx.next_o()
        for h in range(4):
            for ci in range(len(chunk)):
                vr, vb = vfun(u0 + ci, h)
                S.op("pe", lambda e: e.matmul(cx.PS_O[:, osl, h, 0:W], lhsT=E[:, ci, h * 128:(h + 1) * 128], rhs=vr, start=(ci == 0), stop=(ci == len(chunk) - 1)),
                     reads=[bE] + vb, writes=[cx.bPS_O[osl]])
        if DBG_STOP <= 12:
            continue
        if first:
            S.op("dve", lambda e: e.tensor_copy(out=acc[:, :, 0:W], in_=cx.PS_O[:, osl, :, 0:W]), reads=[cx.bPS_O[osl]], writes=[b_acc])
        else:
            S.op("dve", lambda e: e.tensor_tensor(out=acc[:, :, 0:W], in0=acc[:, :, 0:W], in1=cx.PS_O[:, osl, :, 0:W], op=ALU.add),
                 reads=[cx.bPS_O[osl], b_acc], writes=[b_acc])
        first = False


def finalize(cx, acc, b_acc, out_ap, b_out, den_t, b_den, extra=None, coef=None):
    S = cx.sch
    if extra is not None:
        ex, b_ex = extra
        S.op("dve", lambda e: e.tensor_tensor(out=den_t, in0=acc[:, :, 64], in1=ex, op=ALU.add), reads=[b_acc, b_ex], writes=[b_den])
        S.op("dve", lambda e: e.tensor_scalar(out=den_t, in0=den_t, scalar1=1e-30, scalar2=None, op0=ALU.max), reads=[b_den], writes=[b_den])
    else:
        S.op("dve", lambda e: e.tensor_scalar(out=den_t, in0=acc[:, :, 64], scalar1=1e-30, scalar2=None, op0=ALU.max), reads=[b_acc], writes=[b_den])
    S.op("dve", lambda e: e.reciprocal(out=den_t, in_=den_t), reads=[b_den], writes=[b_den])
    if coef is not None:
        cf, b_cf = coef
        S.op("dve", lambda e: e.tensor_tensor(out=den_t, in0=den_t, in1=cf, op=ALU.mult), reads=[b_den, b_cf], writes=[b_den])
    S.op("dve", lambda e: e.tensor_tensor(out=out_ap, in0=acc[:, :, 0:64], in1=den_t.unsqueeze(2).to_broadcast([128, 4, 64]), op=ALU.mult),
         reads=[b_acc, b_den], writes=[b_out])


def mixer_D(cx, KF_all, KV_all, QF_own, sink_d, DW_d, ys_all, b_ys):
    S = cx.sch
    T, NT, SL = cx.T, cx.NT, cx.S
    with ExitStack() as st:
        sb = lambda n, s, d: cx.sb(n, s, d, st)
        DkT = sb("DkT", [64, 2, SL], BF16); b_DkT = Buf("DkT")
        for c in range(2):
            load_kf(cx, KF_all, DkT[:, c, :], b_DkT, 448 + 64 * c, 64)
        Dv = sb("Dv", [128, NT, 2, 65], BF16); b_Dv = Buf("Dv")
        S.op("dve", lambda e: e.memset(Dv[:, :, :, 64:65], 1.0), writes=[b_Dv])
        for c in range(2):
            load_kv(cx, KV_all, Dv[:, :, c, 0:64], b_Dv, 384 + 64 * c, 64)
        DWs = sb("DWs", [128, 5, 512], BF16); b_DW = Buf("DWs")
        S.dma("sp", DWs[:], DW_d.rearrange("w s c -> s w c"), writes=[b_DW])
        esink = sb("esink", [128, 4], F32); b_es = Buf("esink")
        S.dma("sp", esink[:], sink_d[0:1, :].to_broadcast([128, 4]), writes=[b_es])
        S.op("act", lambda e: e.activation(out=esink[:], in_=esink[:], func=AF.Exp), reads=[b_es], writes=[b_es])
        Dq = sb("Dq", [64, 2, 2, 128], BF16); b_Dq = Buf("Dq")
        acc = sb("accD", [128, 4, 65], F32); b_acc = Buf("accD")
        den = sb("denD", [128, 4], F32); b_den = Buf("denD")
        for m in range(T):
            cols = slice(m * 128, (m + 1) * 128)
            for c in range(2):
                S.dma("sp", Dq[:, c, :, :], QF_own[768 + c * 128:768 + (c + 1) * 128, cols].rearrange("(g d) t -> d g t", g=2), writes=[b_Dq])
            units, kts = [], []
            for w in range(5):
                kt = GROUP * m - 1 + w
                if kt < 0:
                    continue
                ks = slice(kt * 128, (kt + 1) * 128)
                masks = [(cx.idb[:], DWs[:, w, :], [cx.b_idb, b_DW])]
                qk = [(0, 256, DkT[:, 0, ks], Dq[:, 0, :, :].rearrange("p g t -> p (g t)"), [b_DkT, b_Dq]),
                      (256, 512, DkT[:, 1, ks], Dq[:, 1, :, :].rearrange("p g t -> p (g t)"), [b_DkT, b_Dq])]
                units.append((masks, qk))
                kts.append(kt)
            if DBG_STOP <= 9:
                continue
            attend(cx, units, lambda ui, h: (Dv[:, kts[ui], h // 2, :], [b_Dv]), 65, acc, b_acc)
            if DBG_STOP <= 13:
                continue
            finalize(cx, acc, b_acc, ys_all[:, m, 768:1024].rearrange("p (h d) -> p h d", h=4), b_ys, den[:], b_den, extra=(esink[:], b_es))
        S.barrier()


def mixer_A(cx, KF_all, KV_all, QF_own, IW_own, AM_d, I4_d, pow2_d, ys_all, b_ys):
    S = cx.sch
    T, NT, SL = cx.T, cx.NT, cx.S
    with ExitStack() as st:
        sb = lambda n, s, d: cx.sb(n, s, d, st)
        AkT = sb("AkT", [64, SL], BF16); b_AkT = Buf("AkT")
        load_kf(cx, KF_all, AkT[:, :], b_AkT, 576, 64)
        ikT = sb("ikT", [32, SL], BF16); b_ikT = Buf("ikT")
        load_kf(cx, KF_all, ikT[:, :], b_ikT, 704, 32)
        Av = sb("Av", [128, NT, 65], BF16); b_Av = Buf("Av")
        S.op("dve", lambda e: e.memset(Av[:, :, 64:65], 1.0), writes=[b_Av])
        load_kv(cx, KV_all, Av[:, :, 0:64], b_Av, 512, 64)
        AMs = sb("AMs", [128, 512], F32); b_AM = Buf("AMs")
        S.dma("sp", AMs[:], AM_d[:, :], writes=[b_AM])
        I4s = sb("I4s", [128, 512], BF16); b_I4 = Buf("I4s")
        S.dma("sp", I4s[:], I4_d[:, :], writes=[b_I4])
        pw = sb("pw", [128, KBIS + 2], F32); b_pw = Buf("pw")
        S.dma("sp", pw[:], pow2_d[:, :], writes=[b_pw])
        score = sb("score", [128, SL], F32); b_sc = Buf("score")
        nb = sb("nb", [128, SL], BF16); b_nb = Buf("nb")
        junkb = sb("junkb", [128, SL], BF16); b_jb = Buf("junkb")
        Aq = sb("Aq", [64, 4, 128], BF16); b_Aq = Buf("Aq")
        iq = sb("iq", [32, 8, 128], BF16); b_iq = Buf("iq")
        wt = sb("wt", [128, 8], F32); b_wt = Buf("wt")
        diag = sb("diag", [128, 8, 128], BF16); b_dg = Buf("diag")
        Rl = [sb("Rl%d" % i, [128, 512], BF16) for i in range(3)]
        b_Rl = [Buf("Rl%d" % i) for i in range(3)]
        Bm = sb("Bm", [128, 1], F32); b_Bm = Buf("Bm")
        Wt = sb("Wt", [128, KBIS + 2], F32); b_Wt = Buf("Wt")
        mid = sb("mid", [128, 1], F32); b_mid = Buf("mid")
        cnt = sb("cnt", [128, 1], F32); b_cnt = Buf("cnt")
        tt = sb("tt", [128, 1], F32); b_tt = Buf("tt")
        acc = sb("accA", [128, 4, 65], F32); b_acc = Buf("accA")
        den = sb("denA", [128, 4], F32); b_den = Buf("denA")
        for m in range(T):
            cols = slice(m * 128, (m + 1) * 128)
            S.dma("sp", Aq[:], QF_own[0:256, cols].rearrange("(h d) t -> d h t", h=4), writes=[b_Aq])
            S.dma("sp", iq[:], QF_own[1024:1280, cols].rearrange("(h d) t -> d h t", h=8), writes=[b_iq])
            S.dma("sp", wt[:], IW_own[cols, :], writes=[b_wt])
            for h in range(8):
                S.op("dve", lambda e: e.tensor_scalar(out=diag[:, h, :], in0=cx.idf[:], scalar1=wt[:, h:h + 1], scalar2=None, op0=ALU.mult),
                     reads=[cx.b_idf, b_wt], writes=[b_dg])
            R = 512 * (m + 1)
            for c in range(m + 1):
                ks = slice(512 * c, 512 * (c + 1))
                pend = None
                for h in range(8):
                    sl = cx.next_s()
                    S.op("pe", lambda e: e.matmul(cx.PS_S[:, sl, :], lhsT=iq[:, h, :], rhs=ikT[:, ks], start=True, stop=True), reads=[b_iq, b_ikT], writes=[cx.bPS_S[sl]])
                    ri = (c * 8 + h) % 3
                    S.op("act", lambda e: e.activation(out=Rl[ri][:], in_=cx.PS_S[:, sl, :], func=AF.Relu), reads=[cx.bPS_S[sl]], writes=[b_Rl[ri]])
                    if pend is not None:
                        ph, pri = pend
                        S.op("pe", lambda e: e.matmul(cx.PS_M[:, :], lhsT=diag[:, ph, :], rhs=Rl[pri][:], start=(ph == 0), stop=False), reads=[b_dg, b_Rl[pri]], writes=[cx.bPS_M])
                    pend = (h, ri)
                ph, pri = pend
                S.op("pe", lambda e: e.matmul(cx.PS_M[:, :], lhsT=diag[:, ph, :], rhs=Rl[pri][:], start=False, stop=True), reads=[b_dg, b_Rl[pri]], writes=[cx.bPS_M])
                S.op("dve", lambda e: e.tensor_copy(out=score[:, ks], in_=cx.PS_M[:, :]), reads=[cx.bPS_M], writes=[b_sc])
            S.op("dve", lambda e: e.tensor_reduce(out=Bm[:], in_=score[:, 0:R], op=ALU.max, axis=AX.X), reads=[b_sc], writes=[b_Bm])
            S.op("dve", lambda e: e.tensor_reduce(out=tt[:], in_=score[:, 0:R], op=ALU.min, axis=AX.X), reads=[b_sc], writes=[b_tt])
            S.op("dve", lambda e: e.scalar_tensor_tensor(out=Bm[:], in0=tt[:], scalar=-1.0, in1=Bm[:], op0=ALU.mult, op1=ALU.max), reads=[b_tt, b_Bm], writes=[b_Bm])
            S.op("dve", lambda e: e.tensor_scalar(out=Bm[:], in0=Bm[:], scalar1=1.0, scalar2=None, op0=ALU.add), reads=[b_Bm], writes=[b_Bm])
            S.op("dve", lambda e: e.tensor_scalar(out=Wt[:], in0=pw[:], scalar1=Bm[:, 0:1], scalar2=None, op0=ALU.mult), reads=[b_pw, b_Bm], writes=[b_Wt])
            S.op("dve", lambda e: e.tensor_tensor(out=score[:, 512 * m:512 * (m + 1)], in0=score[:, 512 * m:512 * (m + 1)], in1=AMs[:], op=ALU.add),
                 reads=[b_sc, b_AM], writes=[b_sc])
            S.op("dve", lambda e: e.memset(mid[:], 0.0), writes=[b_mid])
            for k in range(1, KBIS + 1):
                S.op("dve", lambda e: e.tensor_scalar(out=junkb[:, 0:R], in0=score[:, 0:R], scalar1=mid[:, 0:1], scalar2=None, op0=ALU.is_ge, op1=ALU.add, accum_out=cnt[:]),
                     reads=[b_sc, b_mid], writes=[b_jb, b_cnt])
                S.op("dve", lambda e: e.tensor_scalar(out=tt[:], in0=cnt[:], scalar1=cx.topk - 0.5, scalar2=Wt[:, k:k + 1], op0=ALU.is_ge, op1=ALU.mult),
                     reads=[b_cnt, b_Wt], writes=[b_tt])
                S.op("dve", lambda e: e.tensor_scalar(out=mid[:], in0=tt[:], scalar1=mid[:, 0:1], scalar2=Wt[:, k + 1:k + 2], op0=ALU.add, op1=ALU.subtract),
                     reads=[b_tt, b_mid, b_Wt], writes=[b_mid])
            S.op("dve", lambda e: e.tensor_tensor(out=mid[:], in0=mid[:], in1=Wt[:, KBIS + 1:KBIS + 2], op=ALU.subtract), reads=[b_mid, b_Wt], writes=[b_mid])
            S.op("dve", lambda e: e.tensor_scalar(out=nb[:, 0:R], in0=score[:, 0:R], scalar1=mid[:, 0:1], scalar2=NEGM, op0=ALU.is_lt, op1=ALU.mult),
                 reads=[b_sc, b_mid], writes=[b_nb])
            Aqf = Aq[:, :, :].rearrange("p h t -> p (h t)")
            units = []
            for kt in range(GROUP * (m + 1)):
                ks = slice(kt * 128, (kt + 1) * 128)
                units.append(([(nb[:, ks], I4s[:], [b_nb, b_I4])], [(0, 512, AkT[:, ks], Aqf, [b_AkT, b_Aq])]))
            attend(cx, units, lambda ui, h: (Av[:, ui, :], [b_Av]), 65, acc, b_acc)
            finalize(cx, acc, b_acc, ys_all[:, m, 0:256].rearrange("p (h d) -> p h d", h=4), b_ys, den[:], b_den)
        S.barrier()


def mixer_B(cx, KF_all, KV_all, QF_own, GM_d, OWN_d, EwB_d, DMs, b_DM, ys_all, b_ys):
    S = cx.sch
    T, NT, SL, NB = cx.T, cx.NT, cx.S, cx.NB
    with ExitStack() as st:
        sb = lambda n, s, d: cx.sb(n, s, d, st)
        BkT = sb("BkT", [64, 4, SL], BF16); b_BkT = Buf("BkT")
        for h in range(4):
            load_kf(cx, KF_all, BkT[:, h, :], b_BkT, h * 64, 64)
        Bv = sb("Bv", [128, NT, 4, 65], BF16); b_Bv = Buf("Bv")
        S.op("dve", lambda e: e.memset(Bv[:, :, :, 64:65], 1.0), writes=[b_Bv])
        for h in range(4):
            load_kv(cx, KV_all, Bv[:, :, h, 0:64], b_Bv, 64 * h, 64)
        EwBs = sb("EwBs", [NB, SL], BF16); b_Ew = Buf("EwBs")
        S.dma("sp", EwBs[:], EwB_d[:, :], writes=[b_Ew])
        kmf = sb("kmf", [64, 4 * NB], F32); b_kmf = Buf("kmf")
        kmT = sb("kmT", [64, 4, NB], BF16); b_kmT = Buf("kmT")
        S.op("dve", lambda e: e.tensor_reduce(out=kmf[:], in_=BkT[:, :, :].rearrange("p a (j s) -> p (a j) s", s=256), op=ALU.add, axis=AX.X), reads=[b_BkT], writes=[b_kmf])
        S.op("act", lambda e: e.activation(out=kmT[:, :, :].rearrange("p a j -> p (a j)"), in_=kmf[:], func=AF.Copy, scale=1.0 / 256), reads=[b_kmf], writes=[b_kmT])
        Bq = sb("Bq", [64, 4, 128], BF16); b_Bq = Buf("Bq")
        GMt = sb("GMt", [128, NB], F32); b_GM = Buf("GMt")
        OWt = sb("OWt", [128, NB], F32); b_OW = Buf("OWt")
        gm = sb("gm", [128, 4, NB], F32); b_gm = Buf("gm")
        m8 = sb("m8", [128, 4, 8], F32); b_m8 = Buf("m8")
        sel = sb("sel", [128, 4, NB], F32); b_sel = Buf("sel")
        nsel = sb("nselB", [128, 4, NB], BF16); b_nsel = Buf("nselB")
        nselT = sb("nselTB", [NB, 512], BF16); b_nselT = Buf("nselTB")
        acc = sb("accB", [128, 4, 65], F32); b_acc = Buf("accB")
        den = sb("denB", [128, 4], F32); b_den = Buf("denB")
        for m in range(T):
            cols = slice(m * 128, (m + 1) * 128)
            S.dma("sp", Bq[:], QF_own[256:512, cols].rearrange("(h d) t -> d h t", h=4), writes=[b_Bq])
            S.dma("sp", GMt[:], GM_d[m, :, :], writes=[b_GM])
            S.dma("sp", OWt[:], OWN_d[m, :, :], writes=[b_OW])
            for h in range(4):
                S.op("pe", lambda e: e.matmul(cx.PS_M[:, h * NB:(h + 1) * NB], lhsT=Bq[:, h, :], rhs=kmT[:, h, :], start=True, stop=True),
                     reads=[b_Bq, b_kmT], writes=[cx.bPS_M])
            S.op("dve", lambda e: e.tensor_tensor(out=gm[:], in0=cx.PS_M[:, 0:4 * NB].rearrange("p (h j) -> p h j", h=4), in1=GMt[:, :].unsqueeze(1).to_broadcast([128, 4, NB]), op=ALU.add),
                 reads=[cx.bPS_M, b_GM], writes=[b_gm])
            for h in range(4):
                S.op("dve", lambda e: e.max(out=m8[:, h, :], in_=gm[:, h, :]), reads=[b_gm], writes=[b_m8])
            S.op("dve", lambda e: e.tensor_tensor(out=sel[:], in0=gm[:], in1=m8[:, :, 2:3].to_broadcast([128, 4, NB]), op=ALU.is_ge), reads=[b_gm, b_m8], writes=[b_sel])
            S.op("dve", lambda e: e.tensor_tensor(out=sel[:], in0=sel[:], in1=OWt[:, :].unsqueeze(1).to_broadcast([128, 4, NB]), op=ALU.max), reads=[b_sel, b_OW], writes=[b_sel])
            S.op("dve", lambda e: e.tensor_scalar(out=nsel[:], in0=sel[:], scalar1=-1.0, scalar2=-NEGM, op0=ALU.add, op1=ALU.mult), reads=[b_sel], writes=[b_nsel])
            for h in range(4):
                S.op("pe", lambda e: e.transpose(out=cx.PS_T[0:NB, h * 128:(h + 1) * 128], in_=nsel[:, h, :], identity=cx.idb[:]), reads=[b_nsel, cx.b_idb], writes=[cx.bPS_T])
            S.op("act", lambda e: e.copy(out=nselT[:], in_=cx.PS_T[0:NB, 0:512]), reads=[cx.bPS_T], writes=[b_nselT])
            units = []
            for kt in range(GROUP * (m + 1)):
                ks = slice(kt * 128, (kt + 1) * 128)
                masks = [(EwBs[:, ks], nselT[:], [b_Ew, b_nselT])]
                if kt >= GROUP * m:
                    masks.append((cx.idb[:], DMs[:, kt - GROUP * m, :], [cx.b_idb, b_DM]))
                qk = []
                for h in range(4):
                    qk.append((h * 128, (h + 1) * 128, BkT[:, h, ks], Bq[:, h, :], [b_BkT, b_Bq]))
                units.append((masks, qk))
            attend(cx, units, lambda ui, h: (Bv[:, ui, h, :], [b_Bv]), 65, acc, b_acc)
            finalize(cx, acc, b_acc, ys_all[:, m, 256:512].rearrange("p (h d) -> p h d", h=4), b_ys, den[:], b_den)
        S.barrier()


def mixer_C(cx, KF_all, KV_all, QF_own, CG_own, w1_d, w2_d, peT_d, ckg1_d, CTS_d, CM_d, FIX_d, Ew_d, WM_d, DMs, b_DM, ys_all, b_ys):
    S = cx.sch
    T, NT, SL, NSLC, NCMP, NCT = cx.T, cx.NT, cx.S, cx.NSLC, cx.NCMP, cx.NCT
    Wc = 65 + NSLC
    with ExitStack() as st:
        sb = lambda n, s, d: cx.sb(n, s, d, st)
        kslcT = sb("kslcT", [64, SL], BF16); b_kslc = Buf("kslcT")
        load_kf(cx, KF_all, kslcT[:, :], b_kslc, 320, 64)
        kwinT = sb("kwinT", [64, SL], BF16); b_kwin = Buf("kwinT")
        load_kf(cx, KF_all, kwinT[:, :], b_kwin, 384, 64)
        vslc = sb("vslc", [128, NT, 65], BF16); b_vslc = Buf("vslc")
        S.op("dve", lambda e: e.memset(vslc[:, :, 64:65], 1.0), writes=[b_vslc])
        load_kv(cx, KV_all, vslc[:, :, 0:64], b_vslc, 256, 64)
        vwin = sb("vwin", [128, NT, 65], BF16); b_vwin = Buf("vwin")
        S.op("dve", lambda e: e.memset(vwin[:, :, 64:65], 1.0), writes=[b_vwin])
        load_kv(cx, KV_all, vwin[:, :, 0:64], b_vwin, 320, 64)
        Ews = sb("Ews", [NSLC, SL], BF16); b_Ew = Buf("Ews")
        S.dma("sp", Ews[:], Ew_d[:, :], writes=[b_Ew])
        WMs = sb("WMs", [128, 8, 512], BF16); b_WM = Buf("WMs")
        S.dma("sp", WMs[:], WM_d.rearrange("w s c -> s w c"), writes=[b_WM])
        kcmpT = sb("kcmpT", [64, NCT * 128], BF16); b_kcmp = Buf("kcmpT")
        vcmp = sb("vcmp", [128, NCT, Wc], BF16); b_vcmp = Buf("vcmp")
        S.op("dve", lambda e: e.memset(kcmpT[:], 0.0), writes=[b_kcmp])
        S.op("dve", lambda e: e.memset(vcmp[:], 0.0), writes=[b_vcmp])
        S.op("dve", lambda e: e.memset(vcmp[:, :, 64:65], 1.0), writes=[b_vcmp])
        S.dma("sp", vcmp[:, :, 65:Wc], CTS_d.rearrange("(a n) j -> n a j", n=128), writes=[b_vcmp])

        with ExitStack() as st2:
            sb2 = lambda n, s, d: cx.sb(n, s, d, st2)
            XT = sb2("XT", [64, 2, SL], BF16); b_XT = Buf("XT")
            load_kf(cx, KF_all, XT[:, 0, :], b_XT, 256, 64)
            load_kf(cx, KF_all, XT[:, 1, :], b_XT, 640, 64)
            W1 = sb2("W1", [64, 2, 32, 128], BF16); b_W1 = Buf("W1")
            for e_ in range(2):
                S.dma("pool", W1[:, e_, :, :], w1_d[e_, :, :].rearrange("(l d) h -> d l h", d=64), writes=[b_W1])
            W2 = sb2("W2", [128, 2, 64], BF16); b_W2 = Buf("W2")
            S.dma("pool", W2[:], w2_d.rearrange("e h d -> h e d"), writes=[b_W2])
            peT = sb2("peT", [64, 2, 32], BF16); b_pe = Buf("peT")
            S.dma("pool", peT[:], peT_d.rearrange("e d l -> d e l"), writes=[b_pe])
            g1t = sb2("ckg1", [128, 64], F32); b_g1t = Buf("ckg1")
            S.dma("sp", g1t[:], ckg1_d[0:1, :].to_broadcast([128, 64]), writes=[b_g1t])
            bias = sb2("cbias", [128, 1], F32); b_bias = Buf("cbias")
            z = sb2("cz", [128, NCMP], F32); b_z = Buf("cz")
            z2 = sb2("cz2", [128, NCMP], F32); b_z2 = Buf("cz2")
            glT = sb2("glT", [128, NCMP], BF16); b_gl = Buf("glT")
            ko = sb2("ko", [128, 64], F32); b_ko = Buf("ko")
            kj = sb2("kj", [128, 64], F32); b_kj = Buf("kj")
            kss = sb2("kss", [128, 1], F32); b_kss = Buf("kss")
            krs = sb2("krs", [128, 1], F32); b_krs = Buf("krs")
            kb = sb2("kb", [128, 64], BF16); b_kb = Buf("kb")
            for e_ in range(2):
                for l in range(32):
                    S.op("pe", lambda e: e.matmul(cx.PS_M[:, 0:1], lhsT=W1[:, e_, l, :], rhs=peT[:, e_, l:l + 1], start=(l == 0), stop=(l == 31)), reads=[b_W1, b_pe], writes=[cx.bPS_M])
                S.op("act", lambda e: e.copy(out=bias[:], in_=cx.PS_M[:, 0:1]), reads=[cx.bPS_M], writes=[b_bias])
                XV = XT[:, e_, :].rearrange("d (n s) -> d n s", s=16)
                sl = cx.next_s()
                for l in range(32):
                    S.op("pe", lambda e: e.matmul(cx.PS_S[:, sl, 0:NCMP], lhsT=W1[:, e_, l, :], rhs=XV[:, l // 16:l // 16 + NCMP, l % 16], start=(l == 0), stop=(l == 31)),
                         reads=[b_W1, b_XT], writes=[cx.bPS_S[sl]])
                S.op("act", lambda e: e.activation(out=z[:], in_=cx.PS_S[:, sl, 0:NCMP], func=AF.Identity, bias=bias[:, 0:1]), reads=[cx.bPS_S[sl], b_bias], writes=[b_z])
                S.op("pool", lambda e: e.tensor_tensor(out=z2[:], in0=z[:], in1=z[:], op=ALU.mult), reads=[b_z], writes=[b_z2])
                S.op("dve", lambda e: e.tensor_scalar(out=z2[:], in0=z2[:], scalar1=0.044715, scalar2=1.0, op0=ALU.mult, op1=ALU.add), reads=[b_z2], writes=[b_z2])
                S.op("dve", lambda e: e.tensor_tensor(out=z2[:], in0=z2[:], in1=z[:], op=ALU.mult), reads=[b_z2, b_z], writes=[b_z2])
                S.op("act", lambda e: e.activation(out=z2[:], in_=z2[:], func=AF.Tanh, scale=0.7978845608028654), reads=[b_z2], writes=[b_z2])
                S.op("dve", lambda e: e.scalar_tensor_tensor(out=z2[:], in0=z2[:], scalar=1.0, in1=z[:], op0=ALU.add, op1=ALU.mult), reads=[b_z2, b_z], writes=[b_z2])
                S.op("act", lambda e: e.activation(out=glT[:], in_=z2[:], func=AF.Copy, scale=0.5), reads=[b_z2], writes=[b_gl])
                for nt in range(NCT):
                    rows = min(128, NCMP - nt * 128)
                    S.op("pe", lambda e: e.matmul(cx.PS_M[0:rows, 0:64], lhsT=glT[:, nt * 128:nt * 128 + rows], rhs=W2[:, e_, :], start=True, stop=True), reads=[b_gl, b_W2], writes=[cx.bPS_M])
                    if e_ == 0:
                        S.op("act", lambda e: e.copy(out=ko[0:rows, :], in_=cx.PS_M[0:rows, 0:64]), reads=[cx.bPS_M], writes=[b_ko])
                        S.op("act", lambda e: e.activation(out=kj[0:rows, :], in_=ko[0:rows, :], func=AF.Square, accum_out=kss[0:rows, :]), reads=[b_ko], writes=[b_kj, b_kss])
                        S.op("act", lambda e: e.activation(out=krs[0:rows, :], in_=kss[0:rows, :], func=AF.Sqrt, scale=1.0 / 64, bias=cx.eps_t[0:rows, 0:1]), reads=[b_kss, cx.b_eps], writes=[b_krs])
                        S.op("dve", lambda e: e.reciprocal(out=krs[0:rows, :], in_=krs[0:rows, :]), reads=[b_krs], writes=[b_krs])
                        S.op("dve", lambda e: e.scalar_tensor_tensor(out=kb[0:rows, :], in0=ko[0:rows, :], scalar=krs[0:rows, 0:1], in1=g1t[0:rows, :], op0=ALU.mult, op1=ALU.mult),
                             reads=[b_ko, b_krs, b_g1t], writes=[b_kb])
                        S.op("pe", lambda e: e.transpose(out=cx.PS_T[0:64, 0:rows], in_=kb[0:rows, :], identity=cx.idb[0:rows, 0:rows]), reads=[b_kb, cx.b_idb], writes=[cx.bPS_T])
                        S.op("act", lambda e: e.copy(out=kcmpT[:, nt * 128:nt * 128 + rows], in_=cx.PS_T[0:64, 0:rows]), reads=[cx.bPS_T], writes=[b_kcmp])
                    else:
                        S.op("act", lambda e: e.copy(out=vcmp[0:rows, nt, 0:64], in_=cx.PS_M[0:rows, 0:64]), reads=[cx.bPS_M], writes=[b_vcmp])
            S.barrier()

        Cq = sb("Cq", [64, 4, 128], BF16); b_Cq = Buf("Cq")
        cgt = sb("cgt", [128, 4, 3], F32); b_cg = Buf("cgt")
        CMt = sb("CMt", [128, NCT, 512], BF16); b_CM = Buf("CMt")
        FIXt = sb("FIXt", [128, NSLC], F32); b_FIX = Buf("FIXt")
        accc = sb("accC", [128, 4, Wc], F32); b_accc = Buf("accC")
        accs = sb("accS", [128, 4, 65], F32); b_accs = Buf("accS")
        accw = sb("accW", [128, 4, 65], F32); b_accw = Buf("accW")
        den = sb("denC", [128, 4], F32); b_den = Buf("denC")
        OC = sb("OC", [128, 4, 64], F32); b_OC = Buf("OC")
        OS_ = sb("OS", [128, 4, 64], F32); b_OS = Buf("OS")
        OW = sb("OW", [128, 4, 64], F32); b_OWb = Buf("OW")
        imp = sb("imp", [128, NSLC], F32); b_imp = Buf("imp")
        imp3 = sb("imp3", [128, NSLC], F32); b_imp3 = Buf("imp3")
        m8a = sb("m8a", [128, 8], F32); b_m8a = Buf("m8a")
        m8b = sb("m8b", [128, 8], F32); b_m8b = Buf("m8b")
        nsel = sb("nselC", [128, NSLC], BF16); b_nsel = Buf("nselC")
        nselT4 = sb("nselT4", [NSLC, 4, 128], BF16); b_nT4 = Buf("nselT4")
        for m in range(T):
            cols = slice(m * 128, (m + 1) * 128)
            S.dma("sp", Cq[:], QF_own[512:768, cols].rearrange("(h d) t -> d h t", h=4), writes=[b_Cq])
            S.dma("sp", cgt[:, :, :].rearrange("p h k -> p (h k)"), CG_own[cols, :], writes=[b_cg])
            S.dma("sp", CMt[:, :, :].rearrange("p a c -> p (a c)"), CM_d[m, :, :], writes=[b_CM])
            S.dma("sp", FIXt[:], FIX_d[m, :, :], writes=[b_FIX])
            Cqf = Cq[:, :, :].rearrange("p h t -> p (h t)")
            nct_m = min(NCT, (8 * (GROUP * m + GROUP - 1) + 6) // 128 + 1)
            units = []
            for nt in range(nct_m):
                units.append(([(cx.idb[:], CMt[:, nt, :], [cx.b_idb, b_CM])], [(0, 512, kcmpT[:, nt * 128:(nt + 1) * 128], Cqf, [b_kcmp, b_Cq])]))
            attend(cx, units, lambda ui, h: (vcmp[:, ui, :], [b_vcmp]), Wc, accc, b_accc)
            finalize(cx, accc, b_accc, OC[:], b_OC, den[:], b_den, coef=(cgt[:, :, 0], b_cg))
            S.op("dve", lambda e: e.tensor_scalar(out=den[:], in0=accc[:, :, 64], scalar1=1e-30, scalar2=None, op0=ALU.max), reads=[b_accc], writes=[b_den])
            S.op("dve", lambda e: e.reciprocal(out=den[:], in_=den[:]), reads=[b_den], writes=[b_den])
            S.op("dve", lambda e: e.tensor_scalar(out=imp[:], in0=accc[:, 0, 65:Wc], scalar1=den[:, 0:1], scalar2=None, op0=ALU.mult), reads=[b_accc, b_den], writes=[b_imp])
            for h in range(1, 4):
                S.op("dve", lambda e: e.scalar_tensor_tensor(out=imp[:], in0=accc[:, h, 65:Wc], scalar=den[:, h:h + 1], in1=imp[:], op0=ALU.mult, op1=ALU.add),
                     reads=[b_accc, b_den, b_imp], writes=[b_imp])
            S.op("dve", lambda e: e.tensor_tensor(out=imp[:], in0=imp[:], in1=FIXt[:], op=ALU.add), reads=[b_imp, b_FIX], writes=[b_imp])
            S.op("dve", lambda e: e.max(out=m8a[:], in_=imp[:]), reads=[b_imp], writes=[b_m8a])
            S.op("dve", lambda e: e.match_replace(out=imp3[:], in_to_replace=m8a[:], in_values=imp[:], imm_value=-3.0e38), reads=[b_imp, b_m8a], writes=[b_imp3])
            S.op("dve", lambda e: e.max(out=m8b[:], in_=imp3[:]), reads=[b_imp3], writes=[b_m8b])
            S.op("dve", lambda e: e.tensor_scalar(out=nsel[:], in0=imp[:], scalar1=m8b[:, 7:8], scalar2=NEGM, op0=ALU.is_lt, op1=ALU.mult), reads=[b_imp, b_m8b], writes=[b_nsel])
            S.op("pe", lambda e: e.transpose(out=cx.PS_T[0:NSLC, 0:128], in_=nsel[:], identity=cx.idb[:]), reads=[b_nsel, cx.b_idb], writes=[cx.bPS_T])
            S.op("act", lambda e: e.copy(out=nselT4[:], in_=cx.PS_T[0:NSLC, 0:128].unsqueeze(1).to_broadcast([NSLC, 4, 128])), reads=[cx.bPS_T], writes=[b_nT4])
            nT4f = nselT4[:, :, :].rearrange("p h t -> p (h t)")
            units = []
            for kt in range(GROUP * (m + 1)):
                ks = slice(kt * 128, (kt + 1) * 128)
                masks = [(Ews[:, ks], nT4f, [b_Ew, b_nT4])]
                if kt >= GROUP * m:
                    masks.append((cx.idb[:], DMs[:, kt - GROUP * m, :], [cx.b_idb, b_DM]))
                units.append((masks, [(0, 512, kslcT[:, ks], Cqf, [b_kslc, b_Cq])]))
            attend(cx, units, lambda ui, h: (vslc[:, ui, :], [b_vslc]), 65, accs, b_accs)
            finalize(cx, accs, b_accs, OS_[:], b_OS, den[:], b_den, coef=(cgt[:, :, 1], b_cg))
            units, kts = [], []
            for w in range(8):
                kt = GROUP * m - 4 + w
                if kt < 0:
                    continue
                ks = slice(kt * 128, (kt + 1) * 128)
                units.append(([(cx.idb[:], WMs[:, w, :], [cx.b_idb, b_WM])], [(0, 512, kwinT[:, ks], Cqf, [b_kwin, b_Cq])]))
                kts.append(kt)
            attend(cx, units, lambda ui, h: (vwin[:, kts[ui], :], [b_vwin]), 65, accw, b_accw)
            finalize(cx, accw, b_accw, OW[:], b_OWb, den[:], b_den, coef=(cgt[:, :, 2], b_cg))
            S.op("pool", lambda e: e.tensor_tensor(out=OC[:], in0=OC[:], in1=OS_[:], op=ALU.add), reads=[b_OC, b_OS], writes=[b_OC])
            S.op("dve", lambda e: e.tensor_tensor(out=ys_all[:, m, 512:768].rearrange("p (h d) -> p h d", h=4), in0=OC[:], in1=OW[:], op=ALU.add), reads=[b_OC, b_OWb], writes=[b_ys])
        S.barrier()


def merge(cx, ys_all, b_ys, SG_own, x_own, wbr_d, wout_d, x1_own):
    S = cx.sch
    T = cx.T
    b_x1 = Buf("x1d")
    with ExitStack() as st:
        sb = lambda n, s, d: cx.sb(n, s, d, st)
        wbr = sb("wbr", [128, 8, D], BF16); b_wbr = Buf("wbr")
        S.dma("pool", wbr[:], wbr_d.rearrange("(a p) d -> p a d", p=128), writes=[b_wbr])
        wout = sb("wout", [128, 8, D], BF16); b_wout = Buf("wout")
        S.dma("pool", wout[:], wout_d.rearrange("(a p) d -> p a d", p=128), writes=[b_wout])
        ysT = sb("ysT", [128, 8, 128], BF16); b_ysT = Buf("ysT")
        SGt = sb("SGt", [128, 4096], BF16); b_SGt = Buf("SGt")
        xt = sb("xtm", [128, D], F32); b_xt = Buf("xtm")
        macc = sb("macc", [128, D], F32); b_macc = Buf("macc")
        tmp = sb("mtmp", [128, D], F32); b_tmp = Buf("mtmp")
        mb = sb("mb", [128, D], BF16); b_mb = Buf("mb")
        mT = sb("mT", [128, 8, 128], BF16); b_mT = Buf("mT")
        x1t = sb("x1t", [128, D], F32); b_x1t = Buf("x1t")
        for m in range(T):
            rows = slice(m * 128, (m + 1) * 128)
            S.dma("sp", SGt[:], SG_own[rows, :], writes=[b_SGt])
            S.dma("sp", xt[:], x_own[rows, :], writes=[b_xt])
            for k in range(8):
                S.op("pe", lambda e: e.transpose(out=cx.PS_T[:, k * 128:(k + 1) * 128], in_=ys_all[:, m, k * 128:(k + 1) * 128], identity=cx.idb[:]), reads=[b_ys, cx.b_idb], writes=[cx.bPS_T])
            S.op("act", lambda e: e.copy(out=ysT[:, :, :].rearrange("p a b -> p (a b)"), in_=cx.PS_T[:, :]), reads=[cx.bPS_T], writes=[b_ysT])
            for n in range(4):
                osl = cx.next_o()
                PO = cx.PS_O[:, osl, :, :].rearrange("p a b -> p (a b)")
                for half in range(2):
                    for kc in range(2):
                        S.op("pe", lambda e: e.matmul(PO[:, half * 512:(half + 1) * 512], lhsT=ysT[:, 2 * n + kc, :], rhs=wbr[:, 2 * n + kc, half * 512:(half + 1) * 512], start=(kc == 0), stop=(kc == 1)),
                             reads=[b_ysT, b_wbr], writes=[cx.bPS_O[osl]])
                if n == 0:
                    S.op("dve", lambda e: e.tensor_tensor(out=macc[:], in0=PO, in1=SGt[:, 0:D], op=ALU.mult), reads=[cx.bPS_O[osl], b_SGt], writes=[b_macc])
                else:
                    S.op("dve", lambda e: e.tensor_tensor(out=tmp[:], in0=PO, in1=SGt[:, n * D:(n + 1) * D], op=ALU.mult), reads=[cx.bPS_O[osl], b_SGt], writes=[b_tmp])
                    if n < 3:
                        S.op("pool", lambda e: e.tensor_tensor(out=macc[:], in0=macc[:], in1=tmp[:], op=ALU.add), reads=[b_macc, b_tmp], writes=[b_macc])
                    else:
                        S.op("pool", lambda e: e.tensor_tensor(out=mb[:], in0=macc[:], in1=tmp[:], op=ALU.add), reads=[b_macc, b_tmp], writes=[b_mb])
            for k in range(8):
                S.op("pe", lambda e: e.transpose(out=cx.PS_T[:, k * 128:(k + 1) * 128], in_=mb[:, k * 128:(k + 1) * 128], identity=cx.idb[:]), reads=[b_mb, cx.b_idb], writes=[cx.bPS_T])
            S.op("act", lambda e: e.copy(out=mT[:, :, :].rearrange("p a b -> p (a b)"), in_=cx.PS_T[:, :]), reads=[cx.bPS_T], writes=[b_mT])
            osl = cx.next_o()
            PO = cx.PS_O[:, osl, :, :].rearrange("p a b -> p (a b)")
            for half in range(2):
                for k in range(8):
                    S.op("pe", lambda e: e.matmul(PO[:, half * 512:(half + 1) * 512], lhsT=mT[:, k, :], rhs=wout[:, k, half * 512:(half + 1) * 512], start=(k == 0), stop=(k == 7)),
                         reads=[b_mT, b_wout], writes=[cx.bPS_O[osl]])
            S.op("dve", lambda e: e.tensor_tensor(out=x1t[:], in0=PO, in1=xt[:], op=ALU.add), reads=[cx.bPS_O[osl], b_xt], writes=[b_x1t])
            S.dma("sp", x1_own[rows, :], x1t[:], reads=[b_x1t], writes=[b_x1], sembuf=b_x1t)
        S.barrier()
    return b_x1


def phase_b(cx, d):
    S = cx.sch
    T = cx.T
    with ExitStack() as st:
        cx.e_slot = 0
        cx.ebufs = []
        for i in range(2):
            cx.ebufs.append((cx.sb("E%d" % i, [128, 8, 512], BF16, st), Buf("E%d" % i)))
        ys_all = cx.sb("ys_all", [128, T, D], BF16, st); b_ys = Buf("ys_all")
        DMs = cx.sb("DMs", [128, 4, 512], BF16, st); b_DM = Buf("DMs")
        S.dma("sp", DMs[:], d["DM"].rearrange("w s c -> s w c"), writes=[b_DM])
        MX = os.environ.get("MIXERS", "DCBAM")
        if "D" in MX:
            mixer_D(cx, d["KF_all"], d["KV_all"], d["QF"], d["sink"], d["DW"], ys_all, b_ys)
        if "C" in MX:
          mixer_C(cx, d["KF_all"], d["KV_all"], d["QF"], d["CG"], d["w1"], d["w2"], d["peT"], d["ckg1"], d["CTS"], d["CM"], d["FIX"], d["Ew"], d["WM"], DMs, b_DM, ys_all, b_ys)
        if "B" in MX:
            mixer_B(cx, d["KF_all"], d["KV_all"], d["QF"], d["GM"], d["OWN"], d["EwB"], DMs, b_DM, ys_all, b_ys)
        if "A" in MX:
            mixer_A(cx, d["KF_all"], d["KV_all"], d["QF"], d["IW"], d["AM"], d["I4"], d["pow2"], ys_all, b_ys)
        if "YS" in d:
            b_ysd = Buf("ysd")
            for m in range(T):
                S.dma("sp", d["YS"][m * 128:(m + 1) * 128, :], ys_all[:, m, :], reads=[b_ys], writes=[b_ysd], sembuf=b_ys)
            d["_extra"] = [b_ysd]
        if "M" in MX:
            b_x1 = merge(cx, ys_all, b_ys, d["SG"], d["x_own"], d["wbr"], d["wout"], d["x1"])
        else:
            b_x1 = Buf("none")
        S.barrier()
    return b_x1


def make_masks(S_len, r):
    NT = S_len // 128; T = NT // GROUP; NB = S_len // 256; NSLC = S_len // 64
    NCMP = S_len // 16 - 1; NCT = (NCMP + 127) // 128
    si = np.arange(128)[:, None]; ti = np.arange(128)[None, :]
    rep4 = lambda a: np.tile(a, (1, 4))
    tri = np.where(si <= ti, 0.0, NEGM).astype(np.float32)
    DM = np.stack([rep4(np.zeros((128, 128), np.float32) if w < r else (tri if w == r else np.full((128, 128), NEGM, np.float32))) for w in range(4)])
    def win(nw, back, window):
        out = []
        for w in range(nw):
            diff = 128 * (r + back - w) + ti - si
            out.append(rep4(np.where((diff >= 0) & (diff < window), 0.0, NEGM).astype(np.float32)))
        return np.stack(out)
    WM = win(8, 4, 512)
    DW = win(5, 1, 128)
    CM = np.zeros((T, 128, NCT, 512), np.float32)
    FIX = np.zeros((T, 128, NSLC), np.float32)
    GM = np.zeros((T, 128, NB), np.float32)
    OWN = np.zeros((T, 128, NB), np.float32)
    jj = np.arange(NSLC)[None, :]
    for m in range(T):
        j = GROUP * m + r
        tpos = 128 * j + np.arange(128)
        for nt in range(NCT):
            n = nt * 128 + np.arange(128)
            ok = (n[:, None] < NCMP) & (16 * n[:, None] + 31 <= tpos[None, :])
            CM[m, :, nt, :] = rep4(np.where(ok, 0.0, NEGM).astype(np.float32))
        cur = (tpos // 64)[:, None]
        forced = (jj == 0) | (jj == cur) | (jj == cur - 1)
        f = np.where(forced, 1e9, 0.0)
        f = np.where(jj <= cur, f, -1e30)
        FIX[m] = f
        own = j // 2
        GM[m, :, own:] = -1e30
        OWN[m, :, own] = 1.0
    AM = np.concatenate([np.zeros((128, 128), np.float32) if w < r else (np.where(ti.T >= si.T, 0.0, 0.0) if False else (np.where(np.arange(128)[None, :] <= np.arange(128)[:, None], 0.0, -1e30) if w == r else np.full((128, 128), -1e30))) for w in range(4)], axis=1).astype(np.float32)
    return dict(DM=DM.astype(NPBF), WM=WM.astype(NPBF), DW=DW.astype(NPBF), CM=CM.reshape(T, 128, NCT * 512).astype(NPBF),
                FIX=FIX.astype(np.float32), GM=GM, OWN=OWN, AM=AM)


def make_shared_consts(S_len):
    NB = S_len // 256; NSLC = S_len // 64; NCMP = S_len // 16 - 1; NCT = (NCMP + 127) // 128
    s = np.arange(S_len)[None, :]
    Ew = (s // 64 == np.arange(NSLC)[:, None]).astype(np.float32).astype(NPBF)
    EwB = (s // 256 == np.arange(NB)[:, None]).astype(np.float32).astype(NPBF)
    n = np.arange(NCT * 128)[:, None]; j = np.arange(NSLC)[None, :]
    CTS = ((16 * n < 64 * j + 64) & (16 * n + 32 > 64 * j) & (n < NCMP)).astype(np.float32).astype(NPBF)
    I4 = np.tile(np.eye(128, dtype=np.float32), (1, 4)).astype(NPBF)
    pow2 = np.tile((2.0 ** (1 - np.arange(KBIS + 2))).astype(np.float32)[None, :], (128, 1))
    return dict(Ew=Ew, EwB=EwB, CTS=CTS, I4=I4, pow2=pow2, ident=np.eye(128, dtype=np.float32))


def build_b(S_len, debug_ys=False):
    cx = Ctx(S_len)
    cx.topk = min(256, S_len // 4)
    T, NT, NB, NSLC, NCT = cx.T, cx.NT, cx.NB, cx.NSLC, cx.NCT
    d = {}
    d["KF_all"] = cx.din("KF_all", [GROUP * KF_ROWS, T * 128], BF16)
    d["KV_all"] = cx.din("KV_all", [GROUP * T * 128, KV_COLS], BF16)
    d["QF"] = cx.din("QF", [QF_ROWS, T * 128], BF16)
    d["IW"] = cx.din("IW", [T * 128, 8], F32)
    d["CG"] = cx.din("CG", [T * 128, 12], F32)
    d["SG"] = cx.din("SG", [T * 128, 4096], BF16)
    d["x_own"] = cx.din("x_own", [T * 128, D], F32)
    d["sink"] = cx.din("sink", [1, 4], F32)
    d["w1"] = cx.din("w1", [2, 2048, 128], F32)
    d["w2"] = cx.din("w2", [2, 128, 64], F32)
    d["peT"] = cx.din("peT", [2, 64, 32], F32)
    d["ckg1"] = cx.din("ckg1", [1, 64], F32)
    d["wbr"] = cx.din("wbr", [1024, D], F32)
    d["wout"] = cx.din("wout", [1024, D], F32)
    d["DM"] = cx.din("DM", [4, 128, 512], BF16)
    d["WM"] = cx.din("WM", [8, 128, 512], BF16)
    d["DW"] = cx.din("DW", [5, 128, 512], BF16)
    d["CM"] = cx.din("CM", [T, 128, NCT * 512], BF16)
    d["FIX"] = cx.din("FIX", [T, 128, NSLC], F32)
    d["GM"] = cx.din("GM", [T, 128, NB], F32)
    d["OWN"] = cx.din("OWN", [T, 128, NB], F32)
    d["AM"] = cx.din("AM", [128, 512], F32)
    d["Ew"] = cx.din("Ew", [NSLC, S_len], BF16)
    d["EwB"] = cx.din("EwB", [NB, S_len], BF16)
    d["CTS"] = cx.din("CTS", [NCT * 128, NSLC], BF16)
    d["I4"] = cx.din("I4", [128, 512], BF16)
    d["pow2"] = cx.din("pow2", [128, KBIS + 2], F32)
    d["x1"] = cx.dout("x1", [T * 128, D], F32)
    if debug_ys:
        d["YS"] = cx.dout("YS", [T * 128, D], BF16)
    load_consts(cx, cx.stack)
    make_eps(cx, cx.stack)
    b_x1 = phase_b(cx, d)
    cx.sch.finish([b_x1] + d.get("_extra", []))
    print("phase B instr", cx.sch.cnt, "waits", cx.sch.nwait, "dsems", len(cx.sch.dsems))
    cx.stack.close()
    return cx.nc


def phase_c(cx, x1_own, halo, norm_g, wup_d, cw_d, cb_d, wdn_d, x2_own):
    S = cx.sch
    T = cx.T
    NF = DFF // 128
    b_x2 = Buf("x2d")
    with ExitStack() as st:
        sb = lambda n, s, d: cx.sb(n, s, d, st)
        wup = sb("wup", [128, 8, 2 * DFF], BF16); b_wup = Buf("wup")
        for k in range(8):
            S.dma("pool", wup[:, k, :], wup_d[k * 128:(k + 1) * 128, :], writes=[b_wup])
        wdn = sb("wdn", [128, NF, D], BF16); b_wdn = Buf("wdn")
        for f0 in range(0, NF, 6):
            f1 = min(NF, f0 + 6)
            S.dma("pool", wdn[:, f0:f1, :], wdn_d[f0 * 128:f1 * 128, :].rearrange("(a p) d -> p a d", p=128), writes=[b_wdn])
        g2 = sb("g2", [128, D], F32); b_g2 = Buf("g2")
        S.dma("sp", g2[:], norm_g[0:1, :].to_broadcast([128, D]), writes=[b_g2])
        cw = sb("cw", [128, 2 * NF, 3], F32); b_cw = Buf("cw")
        S.dma("sp", cw[:], cw_d[:, :, :], writes=[b_cw])
        cb = sb("cb", [128, 2 * NF], F32); b_cb = Buf("cb")
        S.dma("sp", cb[:], cb_d[:, :], writes=[b_cb])
        xt = sb("xtc", [128, D], F32); b_xt = Buf("xtc")
        xh = sb("xh", [2, D], F32); b_xh = Buf("xh")
        junk = sb("junkc", [128, D], F32); b_junk = Buf("junkc")
        ss = sb("ssc", [128, 1], F32); b_ss = Buf("ssc")
        rstd = sb("rstdc", [128, 1], F32); b_rstd = Buf("rstdc")
        h2 = sb("h2", [128, D], BF16); b_h2 = Buf("h2")
        hh = sb("hh", [2, D], BF16); b_hh = Buf("hh")
        h2T = sb("h2T", [128, 8, 130], BF16); b_h2T = Buf("h2T")
        aT = sb("aT", [128, NF, 128], BF16); b_aT = Buf("aT")
        x2t = sb("x2t", [128, D], F32); b_x2t = Buf("x2t")
        tmps = []
        for i in range(2):
            tmps.append(dict(g=sb("cg%d" % i, [128, 128], F32), bg=Buf("cg%d" % i), v=sb("cv%d" % i, [128, 128], F32), bv=Buf("cv%d" % i),
                             s=sb("cs%d" % i, [128, 128], F32), bs=Buf("cs%d" % i)))
        for m in range(T):
            rows = slice(m * 128, (m + 1) * 128)
            S.dma("sp", xt[:], x1_own[rows, :], writes=[b_xt])
            S.dma("sp", xh[:], halo[m, :, :], writes=[b_xh])
            rms_rows(cx, xh[:], b_xh, D, rstd[0:2, :], b_rstd, junk[0:2, :], b_junk, ss[0:2, :], b_ss, np_=2)
            S.op("dve", lambda e: e.scalar_tensor_tensor(out=hh[:], in0=xh[:], scalar=rstd[0:2, 0:1], in1=g2[0:2, :], op0=ALU.mult, op1=ALU.mult),
                 reads=[b_xh, b_rstd, b_g2], writes=[b_hh])
            for k in range(8):
                S.op("pe", lambda e: e.transpose(out=cx.PS_T[:, 2 * k:2 * k + 2], in_=hh[0:2, k * 128:(k + 1) * 128], identity=cx.idb[0:2, 0:2]), reads=[b_hh, cx.b_idb], writes=[cx.bPS_T])
            S.op("act", lambda e: e.copy(out=h2T[:, :, 0:2], in_=cx.PS_T[:, 0:16].rearrange("p (a b) -> p a b", b=2)), reads=[cx.bPS_T], writes=[b_h2T])
            rms_rows(cx, xt[:], b_xt, D, rstd[:], b_rstd, junk[:], b_junk, ss[:], b_ss)
            S.op("dve", lambda e: e.scalar_tensor_tensor(out=h2[:], in0=xt[:], scalar=rstd[:, 0:1], in1=g2[:], op0=ALU.mult, op1=ALU.mult),
                 reads=[b_xt, b_rstd, b_g2], writes=[b_h2])
            for k in range(8):
                S.op("pe", lambda e: e.transpose(out=cx.PS_T[:, k * 128:(k + 1) * 128], in_=h2[:, k * 128:(k + 1) * 128], identity=cx.idb[:]), reads=[b_h2, cx.b_idb], writes=[cx.bPS_T])
            S.op("act", lambda e: e.copy(out=h2T[:, :, 2:130], in_=cx.PS_T[:, :].rearrange("p (a b) -> p a b", b=128)), reads=[cx.bPS_T], writes=[b_h2T])
            osl = cx.next_o()
            PO = cx.PS_O[:, osl, :, :].rearrange("p a b -> p (a b)")
            for f in range(NF):
                sl = cx.next_s()
                for gi, fc in enumerate((f, f + NF)):
                    for k in range(8):
                        S.op("pe", lambda e: e.matmul(cx.PS_S[:, sl, gi * 256:gi * 256 + 130], lhsT=wup[:, k, fc * 128:(fc + 1) * 128], rhs=h2T[:, k, :], start=(k == 0), stop=(k == 7)),
                             reads=[b_wup, b_h2T], writes=[cx.bPS_S[sl]])
                tp = tmps[f % 2]
                for gi, fc, dst, bdst in ((0, f, tp["g"], tp["bg"]), (1, f + NF, tp["v"], tp["bv"])):
                    U = cx.PS_S[:, sl, gi * 256:gi * 256 + 130]
                    S.op("act", lambda e: e.activation(out=dst[:], in_=U[:, 0:128], func=AF.Identity, scale=cw[:, fc, 0:1], bias=cb[:, fc:fc + 1]),
                         reads=[cx.bPS_S[sl], b_cw, b_cb], writes=[bdst])
                for gi, fc, dst, bdst in ((0, f, tp["g"], tp["bg"]), (1, f + NF, tp["v"], tp["bv"])):
                    U = cx.PS_S[:, sl, gi * 256:gi * 256 + 130]
                    S.op("dve", lambda e: e.scalar_tensor_tensor(out=dst[:], in0=U[:, 1:129], scalar=cw[:, fc, 1:2], in1=dst[:], op0=ALU.mult, op1=ALU.add),
                         reads=[cx.bPS_S[sl], b_cw, bdst], writes=[bdst])
                    S.op("dve", lambda e: e.scalar_tensor_tensor(out=dst[:], in0=U[:, 2:130], scalar=cw[:, fc, 2:3], in1=dst[:], op0=ALU.mult, op1=ALU.add),
                         reads=[cx.bPS_S[sl], b_cw, bdst], writes=[bdst])
                S.op("act", lambda e: e.activation(out=tp["s"][:], in_=tp["g"][:], func=AF.Silu), reads=[tp["bg"]], writes=[tp["bs"]])
                S.op("dve", lambda e: e.tensor_tensor(out=aT[:, f, :], in0=tp["s"][:], in1=tp["v"][:], op=ALU.mult), reads=[tp["bs"], tp["bv"]], writes=[b_aT])
                for half in range(2):
                    S.op("pe", lambda e: e.matmul(PO[:, half * 512:(half + 1) * 512], lhsT=aT[:, f, :], rhs=wdn[:, f, half * 512:(half + 1) * 512], start=(f == 0), stop=(f == NF - 1)),
                         reads=[b_aT, b_wdn], writes=[cx.bPS_O[osl]])
            S.op("dve", lambda e: e.tensor_tensor(out=x2t[:], in0=PO, in1=xt[:], op=ALU.add), reads=[cx.bPS_O[osl], b_xt], writes=[b_x2t])
            S.dma("sp", x2_own[rows, :], x2t[:], reads=[b_x2t], writes=[b_x2], sembuf=b_x2t)
        S.barrier()
    return b_x2


def build_c(S_len):
    cx = Ctx(S_len)
    T = cx.T
    x1 = cx.din("x1", [T * 128, D], F32)
    halo = cx.din("halo", [T, 2, D], F32)
    norm_g = cx.din("norm_g", [1, D], F32)
    wup = cx.din("wup", [D, 2 * DFF], F32)
    cw = cx.din("cw", [128, 2 * DFF // 128, 3], F32)
    cb = cx.din("cb", [128, 2 * DFF // 128], F32)
    wdn = cx.din("wdn", [DFF, D], F32)
    x2 = cx.dout("x2", [T * 128, D], F32)
    load_consts(cx, cx.stack)
    make_eps(cx, cx.stack)
    b = phase_c(cx, x1, halo, norm_g, wup, cw, cb, wdn, x2)
    cx.sch.finish([b])
    print("phase C instr", cx.sch.cnt, "waits", cx.sch.nwait)
    cx.stack.close()
    return cx.nc


def conv_layout(conv_w, conv_b):
    cw = np.ascontiguousarray(conv_w.T.reshape(-1, 128, 3).transpose(1, 0, 2)).astype(np.float32)
    cb = np.ascontiguousarray(conv_b.reshape(-1, 128).T).astype(np.float32)
    return cw, cb


_PROGS = {}
S_FULL = 8192
N_CORES = 8


def _prog(name, S_len):
    key = (name, S_len)
    if key not in _PROGS:
        _PROGS[key] = {"a": build_a, "b": build_b, "c": build_c}[name](S_len)
    return _PROGS[key]


def _run(nc, in_maps):
    return run_bass_kernel_spmd(nc, in_maps, core_ids=list(range(N_CORES))).results


def forward_unfused(inputs, S_len):
    f32 = lambda a: np.ascontiguousarray(np.asarray(a, dtype=np.float32))
    x = f32(inputs["x"])
    Bn = x.shape[0]
    assert Bn * GROUP == N_CORES and x.shape[1] == S_len
    NT = S_len // 128
    T = NT // GROUP
    depth = np.asarray(inputs["w_in"]).shape[0]
    perm = _perm_cols()
    cs64 = _rope_table(S_len, 64)
    cs32 = _rope_table(S_len, 32)
    toks = [own_tokens(S_len, r) for r in range(GROUP)]
    masks = [make_masks(S_len, r) for r in range(GROUP)]
    shared = make_shared_consts(S_len)
    ident = shared["ident"]
    x_cur = x
    for l in range(depth):
        g = lambda k: f32(inputs[k][l])
        w_in_p = np.ascontiguousarray(g("w_in")[:, perm])
        gains = _gains(g("a_qk_g"), g("b_qk_g"), g("c_qk_g"), g("d_qk_g"))
        in_maps = []
        for c in range(N_CORES):
            b, r = divmod(c, GROUP)
            in_maps.append(dict(x_own=np.ascontiguousarray(x_cur[b, toks[r]]), norm_g=g("norm1_g")[None], w_in_p=w_in_p, gains=gains,
                                lat_g=g("a_lat_g")[None], ikg=g("a_idx_k_g")[None], kv_up=g("a_kv_up"),
                                cs64=np.ascontiguousarray(cs64[toks[r]]), cs32=np.ascontiguousarray(cs32[toks[r]]), ident=ident))
        ra = _run(_prog("a", S_len), in_maps)
        in_maps = []
        for c in range(N_CORES):
            b, r = divmod(c, GROUP)
            d = dict(KF_all=np.concatenate([np.asarray(ra[b * GROUP + rr]["KF"]) for rr in range(GROUP)], 0),
                     KV_all=np.concatenate([np.asarray(ra[b * GROUP + rr]["KV"]) for rr in range(GROUP)], 0),
                     QF=np.asarray(ra[c]["QF"]), IW=np.asarray(ra[c]["IW"]), CG=np.asarray(ra[c]["CG"]), SG=np.asarray(ra[c]["SG"]),
                     x_own=np.ascontiguousarray(x_cur[b, toks[r]]), sink=g("d_sink")[None], w1=g("c_cmp_w1"), w2=g("c_cmp_w2"),
                     peT=np.ascontiguousarray(g("c_cmp_pe").transpose(0, 2, 1)), ckg1=np.ascontiguousarray(g("c_qk_g")[1][None]),
                     wbr=np.ascontiguousarray(g("w_branch").reshape(4 * 256, D)), wout=g("w_out"))
            d.update(masks[r])
            d.update({k: v for k, v in shared.items()})
            in_maps.append(d)
        rb = _run(_prog("b", S_len), in_maps)
        x1 = np.empty_like(x_cur)
        for c in range(N_CORES):
            b, r = divmod(c, GROUP)
            x1[b, toks[r]] = np.asarray(rb[c]["x1"])
        cw, cb = conv_layout(g("conv_w"), g("conv_b"))
        in_maps = []
        for c in range(N_CORES):
            b, r = divmod(c, GROUP)
            halo = np.zeros((T, 2, D), np.float32)
            for m in range(T):
                t0 = 128 * (GROUP * m + r)
                if t0 >= 2:
                    halo[m] = x1[b, t0 - 2:t0]
            in_maps.append(dict(x1=np.ascontiguousarray(x1[b, toks[r]]), halo=halo, norm_g=g("norm2_g")[None], wup=g("w_up"), cw=cw, cb=cb,
                                wdn=g("w_down"), ident=ident))
        rc = _run(_prog("c", S_len), in_maps)
        x2 = np.empty_like(x_cur)
        for c in range(N_CORES):
            b, r = divmod(c, GROUP)
            x2[b, toks[r]] = np.asarray(rc[c]["x2"])
        x_cur = x2
    return x_cur


def kernel(**inputs):
    return forward_unfused(inputs, S_FULL)
```
